# Optimizing a Trainium2 kernel written in Bass

```python
import math
import jax, jax.numpy as jnp
from jax import lax
import numpy as np

D_MODEL = 2048
BATCH = 8
SEQ = 2048
DEPTH = 4
DEC_BATCH = 4
DEC_SEQ = 2048
PAST_LEN = 128

A_HEADS = 8
A_HEAD_DIM = 64
A_WIDTH = A_HEADS * 2 * A_HEAD_DIM
B_PATTERNS = ((128, 1), (512, 4), (2048, 16))
B_GROUPS = len(B_PATTERNS)
B_HEADS = 4
B_HEAD_DIM = 128
B_WIDTH = B_GROUPS * B_HEADS * B_HEAD_DIM
B_OUT = B_HEADS * B_HEAD_DIM
GATE_WIDTH = 2 * D_MODEL
IN_WIDTH = 3 * A_WIDTH + 3 * B_WIDTH + GATE_WIDTH
IN_SPLITS = (A_WIDTH, 2 * A_WIDTH, 3 * A_WIDTH,
             3 * A_WIDTH + B_WIDTH, 3 * A_WIDTH + 2 * B_WIDTH, 3 * A_WIDTH + 3 * B_WIDTH,
             3 * A_WIDTH + 3 * B_WIDTH + D_MODEL)
D_FF = -(-8 * D_MODEL // (3 * 256)) * 256
ROPE_THETA = 500000.0
ROPE_FRAC = 4
Q_BLOCK = 128
NORM_EPS = 1e-6
MASK_VALUE = -1e30

kernel_name = "hybrid_diff_dilated_encoder"


def rms_norm(x, g):
    xf = x.astype(jnp.float32)
    y = xf * lax.rsqrt(jnp.mean(xf * xf, axis=-1, keepdims=True) + NORM_EPS)
    return (y * g.astype(jnp.float32)).astype(x.dtype)


def rope_tables(seq, rot_dim):
    inv = ROPE_THETA ** (-(jnp.arange(0, rot_dim, 2, dtype=jnp.float32) / rot_dim))
    ang = jnp.arange(seq, dtype=jnp.float32)[:, None] * inv[None, :]
    return jnp.cos(ang), jnp.sin(ang)


def apply_partial_rope(x, cos, sin):
    half = cos.shape[-1]
    rot = 2 * half
    x1 = x[..., :half].astype(jnp.float32)
    x2 = x[..., half:rot].astype(jnp.float32)
    c = cos[None, :, None, :]
    s = sin[None, :, None, :]
    rotated = jnp.concatenate([x1 * c - x2 * s, x2 * c + x1 * s], axis=-1).astype(x.dtype)
    return jnp.concatenate([rotated, x[..., rot:]], axis=-1)


def lambda_init_for(layer_idx):
    return 0.8 - 0.6 * math.exp(-0.3 * layer_idx)


def differential_attention(q, k, v, lam, subln_g, lam_init):
    bn, s_len = q.shape[0], q.shape[1]
    nblk = s_len // Q_BLOCK
    scale = A_HEAD_DIM ** -0.5
    qb = q.reshape(bn, nblk, Q_BLOCK, 2 * A_HEADS, A_HEAD_DIM).transpose(1, 0, 2, 3, 4)

    def one_block(qi):
        s = jnp.einsum('bqhd,bkhd->bhqk', qi, k, preferred_element_type=jnp.float32) * scale
        p = jax.nn.softmax(s, axis=-1).reshape(bn, A_HEADS, 2, Q_BLOCK, s_len)
        a = p[:, :, 0] - lam * p[:, :, 1]
        return jnp.einsum('bhqk,bkhe->bqhe', a.astype(v.dtype), v)

    o = lax.map(one_block, qb)
    o = o.transpose(1, 0, 2, 3, 4).reshape(bn, s_len, A_HEADS, 2 * A_HEAD_DIM)
    o = rms_norm(o, subln_g) * (1.0 - lam_init)
    return o.reshape(bn, s_len, A_WIDTH)


def dilated_attention(q, k, v, window, dilation):
    bn, s_len, nh, dh = q.shape
    half = window // (2 * dilation)
    blk = half
    n_sub = s_len // dilation
    nb = -(-n_sub // blk)
    lp = nb * blk
    bb = bn * dilation

    def to_sub(t):
        t = t.reshape(bn, n_sub, dilation, nh, dh).transpose(0, 2, 1, 3, 4)
        return t.reshape(bb, n_sub, nh, dh)

    qs, ks, vs = to_sub(q), to_sub(k), to_sub(v)
    qs = jnp.pad(qs, ((0, 0), (0, lp - n_sub), (0, 0), (0, 0))).reshape(bb, nb, blk, nh, dh)

    def windows(t):
        t = jnp.pad(t, ((0, 0), (blk, lp - n_sub + blk), (0, 0), (0, 0))).reshape(bb, nb + 2, blk, nh, dh)
        return jnp.concatenate([t[:, :-2], t[:, 1:-1], t[:, 2:]], axis=2)

    kw, vw = windows(ks), windows(vs)
    n_idx = jnp.arange(nb)[:, None, None]
    qi = n_idx * blk + jnp.arange(blk)[None, :, None]
    ki = (n_idx - 1) * blk + jnp.arange(3 * blk)[None, None, :]
    mask = (jnp.abs(ki - qi) <= half) & (ki >= 0) & (ki < n_sub)

    s = jnp.einsum('bnqhd,bnkhd->bnhqk', qs, kw, preferred_element_type=jnp.float32) * (dh ** -0.5)
    s = jnp.where(mask[None, :, None], s, MASK_VALUE)
    lse = jax.nn.logsumexp(s, axis=-1)
    p = jnp.exp(s - lse[..., None])
    o = jnp.einsum('bnhqk,bnkhd->bnqhd', p.astype(v.dtype), vw)

    o = o.reshape(bb, lp, nh, dh)[:, :n_sub]
    o = o.reshape(bn, dilation, n_sub, nh, dh).transpose(0, 2, 1, 3, 4).reshape(bn, s_len, nh, dh)
    lse = lse.transpose(0, 1, 3, 2).reshape(bb, lp, nh)[:, :n_sub]
    lse = lse.reshape(bn, dilation, n_sub, nh).transpose(0, 2, 1, 3).reshape(bn, s_len, nh)
    return o, lse


def token_mixer(xn, w_in, gate_bias, lam_p, subln_g, w_pa, w_pb, w_out, lam_init,
                cos_a, sin_a, cos_b, sin_b):
    bn, s_len, _ = xn.shape
    z = jnp.einsum('bsd,de->bse', xn, w_in)
    qa, ka, va, qb, kb, vb, ga, gb = jnp.split(z, IN_SPLITS, axis=-1)

    qa = apply_partial_rope(qa.reshape(bn, s_len, 2 * A_HEADS, A_HEAD_DIM), cos_a, sin_a)
    ka = apply_partial_rope(ka.reshape(bn, s_len, 2 * A_HEADS, A_HEAD_DIM), cos_a, sin_a)
    va = va.reshape(bn, s_len, A_HEADS, 2 * A_HEAD_DIM)
    lf = lam_p.astype(jnp.float32)
    lam = jnp.exp(jnp.sum(lf[0] * lf[1])) - jnp.exp(jnp.sum(lf[2] * lf[3])) + lam_init
    out_a = differential_attention(qa, ka, va, lam, subln_g, lam_init)

    nbh = B_GROUPS * B_HEADS
    qb = apply_partial_rope(qb.reshape(bn, s_len, nbh, B_HEAD_DIM), cos_b, sin_b)
    kb = apply_partial_rope(kb.reshape(bn, s_len, nbh, B_HEAD_DIM), cos_b, sin_b)
    qb = qb.reshape(bn, s_len, B_GROUPS, B_HEADS, B_HEAD_DIM)
    kb = kb.reshape(bn, s_len, B_GROUPS, B_HEADS, B_HEAD_DIM)
    vb = vb.reshape(bn, s_len, B_GROUPS, B_HEADS, B_HEAD_DIM)
    outs, lses = [], []
    for g, (window, dilation) in enumerate(B_PATTERNS):
        o, lse = dilated_attention(qb[:, :, g], kb[:, :, g], vb[:, :, g], window, dilation)
        outs.append(o)
        lses.append(lse)
    wts = jax.nn.softmax(jnp.stack(lses, axis=0), axis=0)
    out_b = jnp.sum(wts[..., None] * jnp.stack(outs, axis=0).astype(jnp.float32), axis=0)
    out_b = out_b.astype(xn.dtype).reshape(bn, s_len, B_OUT)

    ya = jnp.einsum('bse,ed->bsd', out_a, w_pa)
    yb = jnp.einsum('bse,ed->bsd', out_b, w_pb)
    gate_a = jax.nn.sigmoid(ga + gate_bias[0])
    gate_b = jax.nn.sigmoid(gb + gate_bias[1])
    return jnp.einsum('bsd,de->bse', gate_a * ya + gate_b * yb, w_out)


def swiglu(xn, w_ffn_in, w_ffn_out):
    h = jnp.einsum('bsd,df->bsf', xn, w_ffn_in)
    g, u = jnp.split(h, 2, axis=-1)
    return jnp.einsum('bsf,fd->bsd', jax.nn.silu(g) * u, w_ffn_out)


def encoder_trunk(x, norm_mix, norm_ffn, w_in, gate_bias, diff_lambda, diff_subln,
                  w_proj_a, w_proj_b, w_out, w_ffn_in, w_ffn_out, norm_final):
    s_len = x.shape[1]
    cos_a, sin_a = rope_tables(s_len, A_HEAD_DIM // ROPE_FRAC)
    cos_b, sin_b = rope_tables(s_len, B_HEAD_DIM // ROPE_FRAC)
    for l in range(DEPTH):
        xn = rms_norm(x, norm_mix[l])
        x = x + token_mixer(xn, w_in[l], gate_bias[l], diff_lambda[l], diff_subln[l],
                            w_proj_a[l], w_proj_b[l], w_out[l], lambda_init_for(l),
                            cos_a, sin_a, cos_b, sin_b)
        hn = rms_norm(x, norm_ffn[l])
        x = x + swiglu(hn, w_ffn_in[l], w_ffn_out[l])
    return rms_norm(x, norm_final)


def setup_inputs(seed: int = 0) -> dict:
    key = jax.random.key(seed)
    ks = jax.random.split(key, 14)

    def nrm(k, shape, scale):
        return jax.random.normal(k, shape, jnp.float32) * scale

    return {
        'x_prompt': nrm(ks[0], (BATCH, SEQ, D_MODEL), 1.0),
        'x_sample': nrm(ks[1], (DEC_BATCH, DEC_SEQ, D_MODEL), 1.0),
        'norm_mix': 1.0 + nrm(ks[2], (DEPTH, D_MODEL), 0.02),
        'norm_ffn': 1.0 + nrm(ks[3], (DEPTH, D_MODEL), 0.02),
        'w_in': nrm(ks[4], (DEPTH, D_MODEL, IN_WIDTH), D_MODEL ** -0.5),
        'gate_bias': nrm(ks[5], (DEPTH, 2, D_MODEL), 0.1),
        'diff_lambda': nrm(ks[6], (DEPTH, 4, A_HEAD_DIM), 0.1),
        'diff_subln': 1.0 + nrm(ks[7], (DEPTH, 2 * A_HEAD_DIM), 0.02),
        'w_proj_a': nrm(ks[8], (DEPTH, A_WIDTH, D_MODEL), A_WIDTH ** -0.5),
        'w_proj_b': nrm(ks[9], (DEPTH, B_OUT, D_MODEL), B_OUT ** -0.5),
        'w_out': nrm(ks[10], (DEPTH, D_MODEL, D_MODEL), D_MODEL ** -0.5),
        'w_ffn_in': nrm(ks[11], (DEPTH, D_MODEL, 2 * D_FF), D_MODEL ** -0.5),
        'w_ffn_out': nrm(ks[12], (DEPTH, D_FF, D_MODEL), D_FF ** -0.5),
        'norm_final': 1.0 + nrm(ks[13], (D_MODEL,), 0.02),
    }


def reference(x_prompt, x_sample, norm_mix, norm_ffn, w_in, gate_bias, diff_lambda, diff_subln,
              w_proj_a, w_proj_b, w_out, w_ffn_in, w_ffn_out, norm_final):
    y_prompt = encoder_trunk(x_prompt, norm_mix, norm_ffn, w_in, gate_bias, diff_lambda, diff_subln,
                             w_proj_a, w_proj_b, w_out, w_ffn_in, w_ffn_out, norm_final)
    y_sample = encoder_trunk(x_sample, norm_mix, norm_ffn, w_in, gate_bias, diff_lambda, diff_subln,
                             w_proj_a, w_proj_b, w_out, w_ffn_in, w_ffn_out, norm_final)
    return (y_prompt, y_sample)
```

```python
import math
import numpy as np
import concourse.bass as bass
import concourse.mybir as mybir
from concourse.bass_utils import run_bass_kernel_spmd

F32 = mybir.dt.float32
BF16 = mybir.dt.bfloat16
AF = mybir.ActivationFunctionType
ALU = mybir.AluOpType

D = 2048
S = 2048
NL = 4
FF = 5632
KC = 16
FC = 44
EPS = 1e-6
NQK = 40
NG = 32
VW = 2560
B_DIL = (1, 4, 16)
THETA = 500000.0


def lam_init_for(l):
    return 0.8 - 0.6 * math.exp(-0.3 * l)


class Trk:
    def __init__(self, nc):
        self.nc = nc
        self.ops = []
        self.eng = {'pe': nc.tensor, 'act': nc.scalar, 'dve': nc.vector, 'pool': nc.gpsimd, 'sp': nc.sync}

    def add(self, eng, fn, reads=(), writes=(), dma=None):
        self.ops.append((eng, fn, tuple(reads), tuple(writes), dma))

    def fence(self, fn):
        self.ops.append(('sp', fn, ('__ALL__',), (), 'd_fence'))

    def finalize(self):
        nc = self.nc
        ops = self.ops
        n = len(ops)
        deps = [None] * n
        last_w = {}
        last_r = {}
        fence_i = None
        for i, (eng, fn, reads, writes, dma) in enumerate(ops):
            d = set()
            if reads == ('__ALL__',):
                d.update(last_w.values())
                for r in last_r.values():
                    d.update(r.values())
                if fence_i is not None:
                    d.add(fence_i)
                last_w = {}
                last_r = {}
                fence_i = i
                deps[i] = d
                continue
            if fence_i is not None:
                d.add(fence_i)
            for k in reads:
                w = last_w.get(k)
                if w is not None:
                    d.add(w)
            for k in writes:
                w = last_w.get(k)
                if w is not None:
                    d.add(w)
                r = last_r.get(k)
                if r:
                    d.update(r.values())
            for k in writes:
                last_w[k] = i
                last_r[k] = {}
            cls = ('dma', dma) if dma else eng
            for k in reads:
                lr = last_r.get(k)
                if lr is None:
                    lr = last_r[k] = {}
                lr[cls] = i
            d.discard(i)
            if eng == 'pe' and not dma:
                d = {j for j in d if not (ops[j][0] == 'pe' and not ops[j][4])}
            deps[i] = d
        needed = [False] * n
        for i in range(n):
            for j in deps[i]:
                needed[j] = True
        sems = {}

        def sem(name):
            if name not in sems:
                sems[name] = nc.alloc_semaphore(name=name)
            return sems[name]

        val = [0] * n
        semname = [None] * n
        cnt = {}
        for i, (eng, fn, reads, writes, dma) in enumerate(ops):
            if dma:
                cnt[dma] = cnt.get(dma, 0) + 16
                val[i] = cnt[dma]
                semname[i] = dma
            elif needed[i]:
                key = 'E_' + eng
                cnt[key] = cnt.get(key, 0) + 1
                val[i] = cnt[key]
                semname[i] = key
        waited = {e: {} for e in self.eng}
        issued = {}
        for i, (eng, fn, reads, writes, dma) in enumerate(ops):
            e = self.eng[eng]
            need = {}
            for j in deps[i]:
                sn = semname[j]
                if val[j] > need.get(sn, 0):
                    need[sn] = val[j]
            for sn, v in need.items():
                if waited[eng].get(sn, 0) >= v:
                    continue
                if not sn.startswith('E_'):
                    v = issued.get(sn, 0)
                e.wait_ge(sem(sn), v)
                waited[eng][sn] = v
            inst = fn()
            if dma:
                inst.then_inc(sem(dma), 16)
                issued[dma] = issued.get(dma, 0) + 16
            elif needed[i]:
                inst.then_inc(sem(semname[i]), 1)
        sp = self.eng['sp']
        for sn, v in cnt.items():
            if waited['sp'].get(sn, 0) < v:
                sp.wait_ge(sem(sn), v)
        self.nsems = len(sems)
        self.nops = n


class Prog:
    def __init__(self, nl=NL, nseq=2, dbg=False):
        self.nl = nl
        self.nseq = nseq
        nc = self.nc = bass.Bass("TRN2", target_bir_lowering=False)
        self.tr = Trk(nc)
        dt = nc.dram_tensor
        self.x_in = dt("x", [nseq, S, D], F32, kind="ExternalInput").ap()
        self.y_out = dt("y", [nseq, S, D], F32, kind="ExternalOutput").ap()
        self.w_inF = dt("w_inF", [nl, NQK + NG, 128, 2048], F32, kind="ExternalInput").ap()
        self.w_v = dt("w_v", [nl, 5, 128, 8192], F32, kind="ExternalInput").ap()
        self.w_pab = dt("w_pab", [nl, 16, 128, 1536], F32, kind="ExternalInput").ap()
        self.w_o = dt("w_o", [nl, 16, 128, 2048], F32, kind="ExternalInput").ap()
        self.w_f1 = dt("w_f1", [nl, FC, 128, 4096], F32, kind="ExternalInput").ap()
        self.w_f2 = dt("w_f2", [nl, 16, 128, FC * 128], F32, kind="ExternalInput").ap()
        self.NF = 128 + 4 * 16 * nl + 16 + nl
        self.c_f32 = dt("c_f32", [128, self.NF], F32, kind="ExternalInput").ap()
        self.c_dlam = dt("c_dlam", [128, nl * 256], F32, kind="ExternalInput").ap()
        self.c_bf = dt("c_bf", [128, 768], F32, kind="ExternalInput").ap()
        self.c_rope = dt("c_rope", [4, 128, S], F32, kind="ExternalInput").ap()
        self.xT = dt("xT", [nseq, D, S], F32, kind="Internal").ap()
        self.qkT = dt("qkT", [NQK * 128, S], BF16, kind="Internal").ap()
        self.gT = dt("gT", [NG * 128, S], BF16, kind="Internal").ap()
        self.vS = dt("vS", [S, VW], BF16, kind="Internal").ap()
        self._off = (nc.sbuf_base + 63) // 64 * 64
        self._top = nc.sbuf_top
        A = self.alloc
        self.XN = A("XN", [128, 16, S], BF16)
        self.W = A("W", [128, 2, 8192], BF16)
        self.XC = A("XC", [128, 2, 2048], F32)
        self.CF = A("CF", [128, self.NF], F32)
        self.CB = A("CB", [128, 768], BF16)
        self.LAM = A("LAM", [128, 2 * nl], F32)
        self.EPSB = A("EPSB", [128, 1], F32)
        self.FD = A("FD", [128, 16], F32)
        arena0 = self._off
        self.ROPE = A("ROPE", [128, 4, S], F32)
        self.RSTD = A("RSTD", [128, S], F32)
        self.SQ = A("SQ", [128, 2, S], BF16)
        self.ZB = A("ZB", [128, 2, 1024], BF16)
        self.T1 = A("T1", [128, 2, 1024], F32)
        self.T2 = A("T2", [128, 2, 1024], F32)
        self.OST = A("OST", [128, 2, 1024], BF16)
        self.VST = A("VST", [128, 2, 512], BF16)
        end1 = self._off
        self._off = arena0
        self.OA = A("OA", [128, 8, S], BF16)
        self.OB = A("OB", [128, 4, S], BF16)
        qt0 = self._off
        self.QT = A("QT", [128, 2, S], BF16)
        self.KT = A("KT", [128, 2, S], BF16)
        vt0 = self._off
        self.VT = A("VT", [128, 2, 16, 128], BF16)
        sub0 = self._off
        self.PT = A("PT", [128, 4, 512], BF16)
        self.EP = A("EP", [128, 6, 512], F32)
        self.SQA = A("SQA", [128, 512], BF16)
        self.PAB = A("PAB", [128, 2, 512], BF16)
        end2 = self._off
        self._off = vt0
        self.VO = A("VO", [128, 32, 128], BF16)
        self._off = sub0
        self.PB = A("PB", [128, 4, 128], BF16)
        self.OACC = A("OACC", [128, S], F32)
        self.ZACC = A("ZACC", [128, S], F32)
        end3 = self._off
        self._off = qt0
        self.GA = A("GA", [128, 2, 1024], BF16)
        self.GB = A("GB", [128, 2, 1024], BF16)
        self.M1 = A("M1", [128, 2, 1024], F32)
        self.M2 = A("M2", [128, 2, 1024], F32)
        end4 = self._off
        self._off = arena0
        self.ACTB = A("ACTB", [128, FC, 1024], BF16)
        self.SG = A("SG", [128, 1024], F32)
        end5 = self._off
        assert max(end1, end2, end3, end4, end5) <= self._top, (end1, end2, end3, end4, end5, self._top)
        self.PS = nc.alloc_psum_tensor("PS", [128, 8, 512], F32)
        self.wslot = 0
        self.bank = 0
        self.phase = 0

    def alloc(self, name, shape, dtype):
        nbytes = int(np.prod(shape[1:])) * (4 if dtype == F32 else 2)
        t = self.nc.alloc_sbuf_tensor_at(name, shape, dtype, offset=self._off)
        self._off = (self._off + nbytes + 63) // 64 * 64
        return t

    def fence(self):
        nc = self.nc
        self.tr.fence(lambda: nc.sync.dma_start(out=self.FD[0:1, 0:16], in_=self.c_f32[0:1, 0:16]))

    def ak(self, name):
        return name

    def banks(self, n):
        b = self.bank
        if b + n > 8:
            b = 0
        self.bank = (b + n) % 8
        return list(range(b, b + n))

    def ps(self, b0, nb=1):
        if nb == 1:
            return self.PS[:, b0, :]
        return self.PS[:, b0:b0 + nb, :]

    def mm(self, bank, out, lhsT, rhs, start, stop, reads, **kw):
        nc = self.nc
        self.tr.add('pe', lambda: nc.tensor.matmul(out, lhsT=lhsT, rhs=rhs, start=start, stop=stop, **kw),
                    reads=reads, writes=["PS%d" % bank])

    def dma(self, q, out, in_, sem, reads, writes):
        nc = self.nc
        e = {'sp': nc.sync, 'pool': nc.gpsimd}[q]
        self.tr.add(q, lambda: e.dma_start(out=out, in_=in_), reads=reads, writes=writes, dma=sem)

    def act(self, out, in_, func, reads, writes, bias=None, scale=None):
        nc = self.nc
        kw = {}
        if bias is not None:
            kw['bias'] = bias
        if scale is not None:
            kw['scale'] = scale
        self.tr.add('act', lambda: nc.scalar.activation(out=out, in_=in_, func=func, **kw), reads=reads, writes=writes)

    def tt(self, out, in0, in1, op, reads, writes):
        nc = self.nc
        self.tr.add('dve', lambda: nc.vector.tensor_tensor(out=out, in0=in0, in1=in1, op=op), reads=reads, writes=writes)

    def stt(self, out, in0, scalar, in1, op0, op1, reads, writes):
        nc = self.nc
        self.tr.add('dve', lambda: nc.vector.scalar_tensor_tensor(out=out, in0=in0, scalar=scalar, in1=in1, op0=op0, op1=op1),
                    reads=reads, writes=writes)

    def recip(self, out, in_, reads, writes):
        nc = self.nc
        self.tr.add('dve', lambda: nc.vector.reciprocal(out=out, in_=in_), reads=reads, writes=writes)

    def rfast(self, out, in_, reads, writes):
        self.recip(out, in_, reads, writes)

    def vcopy(self, out, in_, reads, writes):
        nc = self.nc
        self.tr.add('dve', lambda: nc.vector.tensor_copy(out=out, in_=in_), reads=reads, writes=writes)

    def ident(self):
        return self.CF[:, 0:128]

    def gvec(self, which, l, c):
        o = 128 + (which * self.nl + l) * 16 + c
        return self.CF[:, o:o + 1]

    def gbias(self, l, ab, c):
        o = 128 + 2 * 16 * self.nl + ((l * 2 + ab) * 16) + c
        return self.CF[:, o:o + 1]

    def gfinal(self, c):
        o = 128 + 4 * 16 * self.nl + c
        return self.CF[:, o:o + 1]

    def gsub_raw(self, l):
        o = 128 + 4 * 16 * self.nl + 16 + l
        return self.CF[:, o:o + 1]

    def setup(self):
        nc, tr, nl = self.nc, self.tr, self.nl
        self.dma('sp', self.CF[:], self.c_f32, 'd_cf', [], ['CF'])
        self.dma('pool', self.CB[:], self.c_bf, 'd_cb', [], ['CB'])
        DL = self.T1[:].rearrange("p a b -> p (a b)")[:, 0:nl * 256]
        PR = self.T2[:].rearrange("p a b -> p (a b)")[:, 0:nl * 128]
        SM = self.T2[:].rearrange("p a b -> p (a b)")[:, 1024:1024 + 2 * nl]
        EX = self.T2[:].rearrange("p a b -> p (a b)")[:, 1100:1100 + 2 * nl]
        self.dma('sp', DL, self.c_dlam, 'd_dl', [], ['DL'])
        dl4 = DL.rearrange("p (l f d) -> p l f d", f=4, d=64)
        pr3 = PR.rearrange("p (l t d) -> p l t d", t=2, d=64)
        for t in range(2):
            self.tt(pr3[:, :, t, :], dl4[:, :, 2 * t, :], dl4[:, :, 2 * t + 1, :], ALU.mult, ['DL'], ['PR%d' % t])
        tr.add('dve', lambda: nc.vector.reduce_sum(out=SM, in_=PR.rearrange("p (a d) -> p a d", d=64), axis=mybir.AxisListType.X),
               reads=['PR0', 'PR1'], writes=['SM'])
        self.act(EX, SM, AF.Exp, ['SM'], ['EX'])
        ex2 = EX.rearrange("p (l t) -> p l t", t=2)
        for l in range(nl):
            li = lam_init_for(l)
            o, i0, i1 = self.LAM[:, l:l + 1], ex2[:, l, 1:2], ex2[:, l, 0:1]
            tr.add('dve', (lambda o=o, i0=i0, i1=i1, li=li: nc.vector.scalar_tensor_tensor(
                out=o, in0=i0, scalar=-li, in1=i1, op0=ALU.add, op1=ALU.subtract)), reads=['EX'], writes=['LAM'])
            o2, g = self.LAM[:, nl + l:nl + l + 1], self.gsub_raw(l)
            tr.add('dve', (lambda o2=o2, g=g, li=li: nc.vector.tensor_scalar(
                out=o2, in0=g, scalar1=1.0 - li, scalar2=None, op0=ALU.mult)), reads=['CF'], writes=['LAM'])

    def stage_in(self, s):
        nc, tr = self.nc, self.tr
        for t in range(17):
            if t < 16:
                sl = t % 2
                self.dma('sp', self.XC[:, sl, :], self.x_in[s, t * 128:(t + 1) * 128, :], 'd_xc%d' % sl, [], ['XC%d' % sl])
            if t > 0:
                tt_ = t - 1
                sl = tt_ % 2
                bs = self.banks(4)
                for c in range(16):
                    b = bs[c // 4]
                    out = self.PS[:, b, (c % 4) * 128:(c % 4 + 1) * 128]
                    in_ = self.XC[:, sl, c * 128:(c + 1) * 128]
                    idn = self.ident()
                    tr.add('pe', (lambda out=out, in_=in_, idn=idn: nc.tensor.transpose(out, in_, idn)),
                           reads=['XC%d' % sl, 'CF'], writes=['PS%d' % b])
                st = tt_ % 2
                stg = self.T1[:, st, :] if False else None
                dst = self.ROPE[:, st, :]
                for q in range(4):
                    o = dst[:, q * 512:(q + 1) * 512]
                    i_ = self.PS[:, bs[q], :]
                    if q % 2 == 0:
                        self.act(o, i_, AF.Copy, ['PS%d' % bs[q]], ['ST0_%d' % st])
                    else:
                        self.vcopy(o, i_, ['PS%d' % bs[q]], ['ST0_%d' % st])
                dr = self.xT[s].rearrange("(c p) t -> p c t", p=128)[:, :, tt_ * 128:(tt_ + 1) * 128]
                self.dma('sp', dr, dst.rearrange("p (c t) -> p c t", t=128), 'd_st0_%d' % st, ['ST0_%d' % st], ['xT%d' % c for c in range(16)])

    def stage_norm(self, s, gfn, t0=0, nt=S, ffn=False):
        nc, tr = self.nc, self.tr
        nb = nt // 512
        bs = self.banks(nb)
        xsrc = self.xT[s].rearrange("(c p) t -> c p t", p=128)
        if ffn:
            rstd, rk = self.SG[:, 0:nt], 'SG'
            sq = [self.XC[:, i, 1024:2048].bitcast(BF16)[:, 0:nt] for i in range(2)]
            sqk = ['XR0', 'XR1']
        else:
            rstd, rk = self.RSTD[:, 0:nt], 'RSTD'
            sq = [self.SQ[:, i, 0:nt] for i in range(2)]
            sqk = ['SQ0', 'SQ1']
        for c in range(17):
            if c < 16:
                sl = c % 2
                self.dma('sp', self.XC[:, sl, 0:nt], xsrc[c, :, t0:t0 + nt], 'd_xc%d' % sl, ['xT%d' % c], ['XC%d' % sl])
            if c > 0:
                cc = c - 1
                sl = cc % 2
                self.act(sq[sl], self.XC[:, sl, 0:nt], AF.Square, ['XC%d' % sl], [sqk[sl]])
                for b in range(nb):
                    self.mm(bs[b], self.PS[:, bs[b], :], self.CB[:, 0:128], sq[sl][:, b * 512:(b + 1) * 512],
                            cc == 0, cc == 15, [sqk[sl], 'CB'])
        psk = ['PS%d' % b for b in bs]
        rps = self.PS[:, bs[0]:bs[0] + nb, :]
        self.act(rstd.rearrange("p (b f) -> p b f", f=512), rps, AF.Sqrt, psk, [rk], bias=self.epsb(), scale=1.0 / D)
        self.rfast(rps, rstd.rearrange("p (b f) -> p b f", f=512), [rk], psk)
        for c in range(17):
            if c < 16:
                sl = c % 2
                self.dma('sp', self.XC[:, sl, 0:nt], xsrc[c, :, t0:t0 + nt], 'd_xc%d' % sl, ['xT%d' % c], ['XC%d' % sl])
            if c > 0:
                cc = c - 1
                sl = cc % 2
                self.stt(self.XN[:, cc, 0:nt].rearrange("p (b f) -> p b f", f=512), self.XC[:, sl, 0:nt].rearrange("p (b f) -> p b f", f=512),
                         gfn(cc), rps, ALU.mult, ALU.mult, ['XC%d' % sl, 'CF'] + psk, ['XN%d' % cc])

    def epsb(self):
        return self.EPSB[:, 0:1]

    def wgroups(self, wd, nch, E):
        G = max(1, 8192 // E)
        R = E
        while R > 2048:
            R //= 2
        if E == FC * 128:
            R = 1408
        assert E % R == 0
        for g0 in range(0, nch, G):
            n = min(G, nch - g0)
            slot = self.wslot
            self.wslot ^= 1

            def load(slot=slot, g0=g0, n=n):
                out = self.W[:, slot, 0:n * E].rearrange("p (n a r) -> p n a r", n=n, r=R)
                in_ = wd[g0:g0 + n].rearrange("n p (a r) -> p n a r", r=R)
                self.dma('pool', out, in_, 'd_w%d' % slot, [], ['W%d' % slot])
            yield load, [(slot, i * E) for i in range(n)]

    def wtile(self, slot, off, kc, width=128, kstride=128):
        o = off + kc * kstride
        return self.W[:, slot, o:o + width]

    def units_p1(self, s, l):
        nc, tr = self.nc, self.tr
        def loadrope():
            self.dma('sp', self.ROPE[:].rearrange("p a t -> p a t"), self.c_rope.rearrange("a p t -> p a t"), 'd_rope', [], ['ROPE'])
        first = [True]
        pend = [None]
        for load, tiles in self.wgroups(self.w_inF[l], NQK + NG, 2048):
            for gi, (slot, off) in enumerate(tiles):
                j = self._p1_j
                self._p1_j += 1

                def comp(j=j, slot=slot, off=off):
                    if j == 0:
                        loadrope()
                    if j < NQK:
                        self.p1_qk(s, l, j, slot, off, pend)
                    else:
                        if pend[0] is not None:
                            pend[0]()
                            pend[0] = None
                        self.p1_gate(s, l, j - NQK, slot, off)
                yield (load if gi == 0 else None), comp
        for load, tiles in self.wgroups(self.w_v[l], 5, 8192):
            for gi, (slot, off) in enumerate(tiles):
                vb = self._p1_v
                self._p1_v += 1

                def compv(vb=vb, slot=slot, off=off):
                    for t in range(16):
                        b = self.banks(1)[0]
                        for kc in range(16):
                            self.mm(b, self.PS[:, b, :], self.XN[:, kc, t * 128:(t + 1) * 128], self.W[:, slot, off + kc * 512:off + (kc + 1) * 512],
                                    kc == 0, kc == 15, ['XN%d' % kc, 'W%d' % slot])
                        st = self._vst
                        self._vst ^= 1
                        if t % 2 == 0:
                            self.act(self.VST[:, st, :], self.PS[:, b, :], AF.Copy, ['PS%d' % b], ['VST%d' % st])
                        else:
                            self.vcopy(self.VST[:, st, :], self.PS[:, b, :], ['PS%d' % b], ['VST%d' % st])
                        self.dma('sp', self.vS[t * 128:(t + 1) * 128, vb * 512:(vb + 1) * 512], self.VST[:, st, :], 'd_vst%d' % st,
                                 ['VST%d' % st], ['vS'])
                yield (load if gi == 0 else None), compv

    def p1_qk(self, s, l, j, slot, off, pend):
        nc, tr = self.nc, self.tr
        isB = j >= 16
        ct = 2 if isB else 0
        perm = self.CB[:, 256:384] if isB else self.CB[:, 128:256]
        grp = None
        if isB:
            jj = (j - 16) % 12
            grp = jj // 4
        for hf in range(2):
            zb_ = self.banks(2)
            for kc in range(16):
                for tb in range(2):
                    self.mm(zb_[tb], self.PS[:, zb_[tb], :], self.wtile(slot, off, kc), self.XN[:, kc, hf * 1024 + tb * 512:hf * 1024 + (tb + 1) * 512],
                            kc == 0, kc == 15, ['XN%d' % kc, 'W%d' % slot])
            if pend[0] is not None:
                pend[0]()
                pend[0] = None
            sl = self._zsl
            self._zsl ^= 1
            zps = self.PS[:, zb_[0]:zb_[0] + 2, :]
            zkeys = ['PS%d' % b for b in zb_]
            tsl = slice(hf * 1024, (hf + 1) * 1024)
            v3 = lambda ap: ap.rearrange("p (b f) -> p b f", f=512)
            self.act(v3(self.ZB[:, sl, :]), zps, AF.Copy, zkeys, ['ZB%d' % sl])
            self.tt(v3(self.T1[:, sl, :]), zps, v3(self.ROPE[:, ct, tsl]), ALU.mult, zkeys + ['ROPE'], ['T1_%d' % sl])

            def rest(j=j, hf=hf, sl=sl, perm=perm, ct=ct, grp=grp, tsl=tsl, v3=v3):
                pb_ = self.banks(2)
                for tb in range(2):
                    self.mm(pb_[tb], self.PS[:, pb_[tb], :], perm, self.ZB[:, sl, tb * 512:(tb + 1) * 512], True, True, ['ZB%d' % sl, 'CB'])
                pps = self.PS[:, pb_[0]:pb_[0] + 2, :]
                self.tt(v3(self.T2[:, sl, :]), pps, v3(self.ROPE[:, ct + 1, tsl]), ALU.mult, ['PS%d' % b for b in pb_] + ['ROPE'], ['T2_%d' % sl])
                o = self.OST[:, sl, :]
                if grp is None or B_DIL[grp] == 1:
                    oo = o
                    dst = self.qkT[j * 128:(j + 1) * 128, hf * 1024:(hf + 1) * 1024]
                    self.tt(oo, self.T1[:, sl, :], self.T2[:, sl, :], ALU.add, ['T1_%d' % sl, 'T2_%d' % sl], ['OST%d' % sl])
                    self.dma('sp', dst, o, 'd_ost%d' % sl, ['OST%d' % sl], ['qkT'])
                else:
                    d = B_DIL[grp]
                    oo = o.rearrange("p (q m) -> p m q", q=d)
                    i1 = self.T1[:, sl, :].rearrange("p (m q) -> p m q", q=d)
                    i2 = self.T2[:, sl, :].rearrange("p (m q) -> p m q", q=d)
                    self.tt(oo, i1, i2, ALU.add, ['T1_%d' % sl, 'T2_%d' % sl], ['OST%d' % sl])
                    nsub = S // d
                    hl = 1024 // d
                    dst = self.qkT[j * 128:(j + 1) * 128, :].rearrange("r (q m) -> r q m", q=d)[:, :, hf * hl:(hf + 1) * hl]
                    self.dma('sp', dst, o.rearrange("p (q m) -> p q m", q=d), 'd_ost%d' % sl, ['OST%d' % sl], ['qkT'])
            pend[0] = rest

    def p1_gate(self, s, l, g, slot, off):
        ab, c = g // 16, g % 16
        for hf in range(2):
            zb_ = self.banks(2)
            for kc in range(16):
                for tb in range(2):
                    self.mm(zb_[tb], self.PS[:, zb_[tb], :], self.wtile(slot, off, kc), self.XN[:, kc, hf * 1024 + tb * 512:hf * 1024 + (tb + 1) * 512],
                            kc == 0, kc == 15, ['XN%d' % kc, 'W%d' % slot])
            sl = self._zsl
            self._zsl ^= 1
            self.act(self.OST[:, sl, :].rearrange("p (b f) -> p b f", f=512), self.PS[:, zb_[0]:zb_[0] + 2, :], AF.Sigmoid,
                     ['PS%d' % b for b in zb_] + ['CF'], ['OST%d' % sl], bias=self.gbias(l, ab, c))
            self.dma('sp', self.gT[g * 128:(g + 1) * 128, hf * 1024:(hf + 1) * 1024], self.OST[:, sl, :], 'd_ost%d' % sl, ['OST%d' % sl], ['gT'])

    def stage_attn_a(self, s, l):
        nc, tr, nl = self.nc, self.tr, self.nl
        scale = 64 ** -0.5
        neglam = self.LAM[:, l:l + 1]
        gsub = self.LAM[:, nl + l:nl + l + 1]
        ones = self.CB[:, 0:128]
        E = self.EP
        OC = [E[:, 0, :], E[:, 1, :]]
        RC = [E[:, 2, :], E[:, 3, :]]

        def loads(h):
            sl = h % 2
            self.dma('sp', self.QT[:, sl, :], self.qkT[h * 128:(h + 1) * 128, :], 'd_qt%d' % sl, ['qkT'], ['QT%d' % sl])
            self.dma('sp', self.KT[:, sl, :], self.qkT[(8 + h) * 128:(9 + h) * 128, :], 'd_kt%d' % sl, ['qkT'], ['KT%d' % sl])
            self.dma('sp', self.VT[:, sl, :, :], self.vS[:, h * 128:(h + 1) * 128].rearrange("(t p) e -> p t e", p=128), 'd_vt%d' % sl,
                     ['vS'], ['VT%d' % sl])
        loads(0)
        pend = [None]
        for h in range(8):
            if h + 1 < 8:
                loads(h + 1)
            sl = h % 2
            qk = ['QT%d' % sl, 'KT%d' % sl]
            for qb in range(4):
                qs = slice(qb * 512, (qb + 1) * 512)

                def s_step(kt):
                    sb = 2 * (kt % 2)
                    for c in range(2):
                        self.mm(sb + c, self.PS[:, sb + c, :], self.KT[64 * c:64 * c + 64, sl, kt * 128:(kt + 1) * 128],
                                self.QT[64 * c:64 * c + 64, sl, qs], True, True, qk)
                    self.act(self.PT[:, sb:sb + 2, :], self.PS[:, sb:sb + 2, :], AF.Exp, ['PS%d' % sb, 'PS%d' % (sb + 1)],
                             ['PT%d' % sb, 'PT%d' % (sb + 1)], scale=scale)

                def av_step(kt):
                    for c in range(2):
                        p = (kt % 2) * 2 + c
                        self.mm(4 + 2 * c, self.PS[:, 4 + 2 * c, :], self.VT[:, sl, kt, :], self.PT[:, p, :], kt == 0, kt == 15, ['VT%d' % sl, 'PT%d' % p])
                        self.mm(5 + 2 * c, self.PS[:, 5 + 2 * c, :], ones, self.PT[:, p, :], kt == 0, kt == 15, ['CB', 'PT%d' % p])
                s_step(0)
                for kt in range(16):
                    if kt + 1 < 16:
                        s_step(kt + 1)
                    av_step(kt)
                    if kt == 4 and pend[0] is not None:
                        pend[0]()
                        pend[0] = None
                for c in range(2):
                    self.vcopy(RC[c], self.PS[:, 5 + 2 * c, :], ['PS%d' % (5 + 2 * c)], ['RC%d' % c])
                    self.act(OC[c], self.PS[:, 4 + 2 * c, :], AF.Copy, ['PS%d' % (4 + 2 * c)], ['OC%d' % c])
                for c in range(2):
                    self.recip(RC[c], RC[c], ['RC%d' % c], ['RC%d' % c])
                    self.tt(OC[c], OC[c], RC[c], ALU.mult, ['OC%d' % c, 'RC%d' % c], ['OC%d' % c])
                self.stt(OC[0], OC[1], neglam, OC[0], ALU.mult, ALU.add, ['OC0', 'OC1', 'LAM'], ['OC0'])
                self.tt(self.SQA[:, :], OC[0], OC[0], ALU.mult, ['OC0'], ['SQA'])

                def part2(h=h, qs=qs):
                    self.mm(0, self.PS[:, 0, :], ones, self.SQA[:, :], True, True, ['SQA', 'CB'])
                    self.act(OC[1], self.PS[:, 0, :], AF.Sqrt, ['PS0'], ['OC1'], bias=self.epsb(), scale=1.0 / 128)
                    self.recip(OC[1], OC[1], ['OC1'], ['OC1'])
                    self.stt(self.OA[:, h, qs], OC[0], gsub, OC[1], ALU.mult, ALU.mult, ['OC0', 'OC1', 'LAM'], ['OA%d' % h])
                pend[0] = part2
        pend[0]()

    def stage_attn_b(self, s, l):
        nc, tr = self.nc, self.tr
        scale = 128 ** -0.5
        ones = self.CB[:, 0:128]
        mHi, mLo, mFi = self.CB[:, 384:512], self.CB[:, 512:640], self.CB[:, 640:768]
        mBoth = self.CB[:, 384:640]
        VOs = [self.VO, self.W[:, 1, 0:4096].rearrange("p (t e) -> p t e", e=128)]
        combos = [(h, g) for h in range(4) for g in range(3)]

        def loads(idx):
            h, g = combos[idx]
            d = B_DIL[g]
            nsub = S // d
            J = nsub // 128
            jq = 16 + g * 4 + h
            jk = 28 + g * 4 + h
            sl = idx % 2
            self.dma('sp', self.QT[:, sl, :], self.qkT[jq * 128:(jq + 1) * 128, :], 'd_qt%d' % sl, ['qkT'], ['QT%d' % sl])
            self.dma('sp', self.KT[:, sl, :], self.qkT[jk * 128:(jk + 1) * 128, :], 'd_kt%d' % sl, ['qkT'], ['KT%d' % sl])
            vcol = 1024 + (g * 4 + h) * 128
            vsrc = self.vS[:, vcol:vcol + 128]
            VO = VOs[sl]
            nt_ = J + 1
            vk, vsem = 'VO%d' % sl, 'd_vo%d' % sl
            for p in range(d):
                if J > 1:
                    src = vsrc[(64 * d + p):(64 * d + p) + ((J - 1) * 128 - 1) * d + 1:d, :].rearrange("(j k) e -> k j e", k=128)
                    self.dma('sp', VO[:, p * nt_ + 1:p * nt_ + J, :], src, vsem, ['vS'], [vk])
            srcf = vsrc[0:64 * d, :].rearrange("(k p) e -> k p e", p=d)
            self.dma('sp', VO[0:64, 0:(d - 1) * nt_ + 1:nt_, :], srcf, vsem, ['vS'], [vk])
            srcl = vsrc[(nsub - 64) * d:nsub * d, :].rearrange("(k p) e -> k p e", p=d)
            self.dma('sp', VO[0:64, J:J + (d - 1) * nt_ + 1:nt_, :], srcl, vsem, ['vS'], [vk])

        loads(0)
        for idx, (h, g) in enumerate(combos):
            if idx + 1 < len(combos):
                loads(idx + 1)
            d = B_DIL[g]
            nsub = S // d
            J = nsub // 128
            sl = idx % 2
            VO = VOs[sl]
            vk = 'VO%d' % sl
            nt_ = J + 1
            qk = ['QT%d' % sl, 'KT%d' % sl]
            blocks = [(p, u) for p in range(d) for u in range(J)]
            info = {}

            def st1(i):
                p, u = blocks[i]
                q0 = p * nsub + u * 128
                sb = self.banks(1)[0]
                tiles = []
                for ti, j in enumerate((u, u + 1)):
                    if j == 0:
                        k0, M, mask = p * nsub, 64, mFi[0:64, :]
                    elif j == J:
                        k0, M, mask = p * nsub + nsub - 64, 64, mLo[0:64, :]
                    else:
                        k0, M, mask = p * nsub + 128 * j - 64, 128, (mHi if ti == 0 else mLo)
                    tiles.append((j, k0, M, mask))
                for ti, (j, k0, M, mask) in enumerate(tiles):
                    self.mm(sb, self.PS[0:M, sb, ti * 128:(ti + 1) * 128], self.KT[:, sl, k0:k0 + M], self.QT[:, sl, q0:q0 + 128], True, True, qk)
                pp = self._pb
                self._pb = (self._pb + 2) % 4
                pks = ['PB%d' % pp, 'PB%d' % (pp + 1)]
                if tiles[0][2] == 128 and tiles[1][2] == 128:
                    pb2 = self.PB[:, pp:pp + 2, :]
                    self.act(pb2, self.PS[:, sb, 0:256].rearrange("p (a b) -> p a b", b=128), AF.Exp, ['PS%d' % sb], pks, scale=scale)
                    self.tt(pb2, pb2, mBoth.rearrange("p (a b) -> p a b", b=128), ALU.mult, pks + ['CB'], pks)
                else:
                    for ti, (j, k0, M, mask) in enumerate(tiles):
                        self.act(self.PB[0:M, pp + ti, :], self.PS[0:M, sb, ti * 128:(ti + 1) * 128], AF.Exp, ['PS%d' % sb], [pks[ti]], scale=scale)
                        self.tt(self.PB[0:M, pp + ti, :], self.PB[0:M, pp + ti, :], mask, ALU.mult, [pks[ti], 'CB'], [pks[ti]])
                info[i] = (tiles, pp)

            def st2(i):
                p, u = blocks[i]
                tiles, pp = info.pop(i)
                ob = self.banks(1)[0]
                for ti, (j, k0, M, mask) in enumerate(tiles):
                    self.mm(ob, self.PS[:, ob, 0:128], VO[0:M, p * nt_ + j, :], self.PB[0:M, pp + ti, :], ti == 0, ti == 1, [vk, 'PB%d' % (pp + ti)])
                for ti, (j, k0, M, mask) in enumerate(tiles):
                    self.mm(ob, self.PS[:, ob, 128:256], ones[0:M, :], self.PB[0:M, pp + ti, :], ti == 0, ti == 1, ['CB', 'PB%d' % (pp + ti)])
                st = (u * 128) * d + p
                oa = self.OACC[:, st:st + 127 * d + 1:d]
                za = self.ZACC[:, st:st + 127 * d + 1:d]
                if g == 0:
                    self.vcopy(oa, self.PS[:, ob, 0:128], ['PS%d' % ob], ['OACC'])
                    self.vcopy(za, self.PS[:, ob, 128:256], ['PS%d' % ob], ['ZACC'])
                else:
                    self.tt(oa, self.PS[:, ob, 0:128], oa, ALU.add, ['PS%d' % ob, 'OACC'], ['OACC'])
                    self.tt(za, self.PS[:, ob, 128:256], za, ALU.add, ['PS%d' % ob, 'ZACC'], ['ZACC'])

            st1(0)
            for i in range(len(blocks)):
                if i + 1 < len(blocks):
                    st1(i + 1)
                st2(i)
            if g == 2:
                self.rfast(self.ZACC[:, :], self.ZACC[:, :], ['ZACC'], ['ZACC'])
                self.tt(self.OB[:, h, :], self.OACC[:, :], self.ZACC[:, :], ALU.mult, ['OACC', 'ZACC'], ['OB%d' % h])

    def units_p2(self, s, l):
        for load, tiles in self.wgroups(self.w_pab[l], 16, 1536):
            for gi, (slot, off) in enumerate(tiles):
                c = self._p2_c
                self._p2_c += 1

                def comp(c=c, slot=slot, off=off):
                    for hf in range(2):
                        sl = self._gsl
                        self._gsl ^= 1
                        ts_ = slice(hf * 1024, (hf + 1) * 1024)
                        self.dma('sp', self.GA[:, sl, :], self.gT[c * 128:(c + 1) * 128, ts_], 'd_ga%d' % sl, ['gT'], ['GA%d' % sl])
                        self.dma('sp', self.GB[:, sl, :], self.gT[(16 + c) * 128:(17 + c) * 128, ts_], 'd_gb%d' % sl, ['gT'], ['GB%d' % sl])
                        ya = self.banks(2)
                        for kc in range(8):
                            for tb in range(2):
                                self.mm(ya[tb], self.PS[:, ya[tb], :], self.wtile(slot, off, kc), self.OA[:, kc, hf * 1024 + tb * 512:hf * 1024 + (tb + 1) * 512],
                                        kc == 0, kc == 7, ['OA%d' % kc, 'W%d' % slot])
                        yb = self.banks(2)
                        for kc in range(4):
                            for tb in range(2):
                                self.mm(yb[tb], self.PS[:, yb[tb], :], self.wtile(slot, off, 8 + kc), self.OB[:, kc, hf * 1024 + tb * 512:hf * 1024 + (tb + 1) * 512],
                                        kc == 0, kc == 3, ['OB%d' % kc, 'W%d' % slot])
                        v3 = lambda ap: ap.rearrange("p (b f) -> p b f", f=512)
                        self.tt(v3(self.M1[:, sl, :]), self.PS[:, ya[0]:ya[0] + 2, :], v3(self.GA[:, sl, :]), ALU.mult,
                                ['PS%d' % b for b in ya] + ['GA%d' % sl], ['M1_%d' % sl])
                        self.tt(v3(self.M2[:, sl, :]), self.PS[:, yb[0]:yb[0] + 2, :], v3(self.GB[:, sl, :]), ALU.mult,
                                ['PS%d' % b for b in yb] + ['GB%d' % sl], ['M2_%d' % sl])
                        self.tt(self.XN[:, c, ts_], self.M1[:, sl, :], self.M2[:, sl, :], ALU.add, ['M1_%d' % sl, 'M2_%d' % sl], ['XN%d' % c])
                yield (load if gi == 0 else None), comp
        yield from self.units_resid(s, self.w_o[l], 2048, 16, self.XN, 'XN', 0, S)

    def units_resid(self, s, wd, E, Kc, act, actkey, t0, nt):
        xsrc = self.xT[s].rearrange("(c p) t -> c p t", p=128)
        nh = nt // 1024
        for load, tiles in self.wgroups(wd, 16, E):
            for gi, (slot, off) in enumerate(tiles):
                c = self._r_c % 16
                self._r_c += 1

                def comp(c=c, slot=slot, off=off):
                    for hf in range(nh):
                        sl = self._xsl
                        self._xsl ^= 1
                        ts_ = slice(t0 + hf * 1024, t0 + (hf + 1) * 1024)
                        self.dma('sp', self.XC[:, sl, 0:1024], xsrc[c, :, ts_], 'd_xc%d' % sl, ['xT%d' % c], ['XC%d' % sl])
                        zb_ = self.banks(2)
                        for kc in range(Kc):
                            for tb in range(2):
                                self.mm(zb_[tb], self.PS[:, zb_[tb], :], self.wtile(slot, off, kc), act[:, kc, hf * 1024 + tb * 512:hf * 1024 + (tb + 1) * 512],
                                        kc == 0, kc == Kc - 1, ['%s%d' % (actkey, kc), 'W%d' % slot])
                        self.tt(self.XC[:, sl, 1024:2048].rearrange("p (b f) -> p b f", f=512), self.PS[:, zb_[0]:zb_[0] + 2, :],
                                self.XC[:, sl, 0:1024].rearrange("p (b f) -> p b f", f=512), ALU.add,
                                ['PS%d' % b for b in zb_] + ['XC%d' % sl], ['XR%d' % sl])
                        self.dma('sp', xsrc[c, :, ts_], self.XC[:, sl, 1024:2048], 'd_xr%d' % sl, ['XR%d' % sl], ['xT%d' % c])
                yield (load if gi == 0 else None), comp

    def units_ffn(self, s, l, blk):
        t0 = blk * 1024
        for load, tiles in self.wgroups(self.w_f1[l], FC, 4096):
            for gi, (slot, off) in enumerate(tiles):
                j = self._f_j % FC
                self._f_j += 1

                def comp(j=j, slot=slot, off=off):
                    gb_ = self.banks(2)
                    for kc in range(16):
                        for tb in range(2):
                            self.mm(gb_[tb], self.PS[:, gb_[tb], :], self.wtile(slot, off, kc), self.XN[:, kc, tb * 512:(tb + 1) * 512],
                                    kc == 0, kc == 15, ['XN%d' % kc, 'W%d' % slot])
                    ub_ = self.banks(2)
                    for kc in range(16):
                        for tb in range(2):
                            self.mm(ub_[tb], self.PS[:, ub_[tb], :], self.wtile(slot, off + 2048, kc), self.XN[:, kc, tb * 512:(tb + 1) * 512],
                                    kc == 0, kc == 15, ['XN%d' % kc, 'W%d' % slot])
                    sg = self.SG[:, :].rearrange("p (b f) -> p b f", f=512)
                    self.act(sg, self.PS[:, gb_[0]:gb_[0] + 2, :], AF.Silu, ['PS%d' % b for b in gb_], ['SG'])
                    self.tt(self.ACTB[:, j, :].rearrange("p (b f) -> p b f", f=512), self.PS[:, ub_[0]:ub_[0] + 2, :], sg, ALU.mult,
                            ['PS%d' % b for b in ub_] + ['SG'], ['ACTB%d' % j])
                yield (load if gi == 0 else None), comp
        yield from self.units_resid(s, self.w_f2[l], FC * 128, FC, self.ACTB, 'ACTB', t0, 1024)

    def stage_out(self, s):
        nc, tr = self.nc, self.tr
        xsrc = self.xT[s].rearrange("(c p) t -> c p t", p=128)
        bs = self.banks(4)
        for c in range(17):
            if c < 16:
                sl = c % 2
                self.dma('sp', self.XC[:, sl, :], xsrc[c], 'd_xc%d' % sl, ['xT%d' % c], ['XC%d' % sl])
            if c > 0:
                cc = c - 1
                sl = cc % 2
                self.act(self.SQ[:, sl, :], self.XC[:, sl, :], AF.Square, ['XC%d' % sl], ['SQ%d' % sl])
                for b in range(4):
                    self.mm(bs[b], self.PS[:, bs[b], :], self.CB[:, 0:128], self.SQ[:, sl, b * 512:(b + 1) * 512], cc == 0, cc == 15, ['SQ%d' % sl, 'CB'])
        self.act(self.RSTD[:].rearrange("p (b f) -> p b f", f=512), self.PS[:, bs[0]:bs[0] + 4, :], AF.Sqrt,
                 ['PS%d' % b for b in bs], ['RSTD'], bias=self.epsb(), scale=1.0 / D)
        self.recip(self.RSTD[:], self.RSTD[:], ['RSTD'], ['RSTD'])
        YN = self.ROPE
        for c in range(17):
            if c < 16:
                sl = c % 2
                self.dma('sp', self.XC[:, sl, :], xsrc[c], 'd_xc%d' % sl, ['xT%d' % c], ['XC%d' % sl])
            if c > 0:
                cc = c - 1
                sl = cc % 2
                self.stt(YN[:, sl, :], self.XC[:, sl, :], self.gfinal(cc), self.RSTD[:], ALU.mult, ALU.mult, ['XC%d' % sl, 'RSTD', 'CF'], ['YN%d' % sl])
                tb_ = self.banks(4)
                for t in range(16):
                    b = tb_[t // 4]
                    out = self.PS[:, b, (t % 4) * 128:(t % 4 + 1) * 128]
                    in_ = YN[:, sl, t * 128:(t + 1) * 128]
                    idn = self.ident()
                    tr.add('pe', (lambda out=out, in_=in_, idn=idn: nc.tensor.transpose(out, in_, idn)), reads=['YN%d' % sl, 'CF'], writes=['PS%d' % b])
                dst = YN[:, 2 + sl, :]
                for q in range(4):
                    o = dst[:, q * 512:(q + 1) * 512]
                    if q % 2 == 0:
                        self.act(o, self.PS[:, tb_[q], :], AF.Copy, ['PS%d' % tb_[q]], ['YT%d' % sl])
                    else:
                        self.vcopy(o, self.PS[:, tb_[q], :], ['PS%d' % tb_[q]], ['YT%d' % sl])
                dr = self.y_out[s].rearrange("(t p) f -> p t f", p=128)[:, :, cc * 128:(cc + 1) * 128]
                self.dma('sp', dr, dst.rearrange("p (t f) -> p t f", f=128), 'd_yt%d' % sl, ['YT%d' % sl], ['y'])

    def prime(self, gen):
        load, comp = next(gen)
        if load is not None:
            load()
        return comp, gen

    def run_units(self, gen, primed=None):
        prev = None
        if primed is not None:
            prev, gen = primed
        for load, comp in (gen if gen is not None else ()):
            if load is not None:
                load()
            if prev is not None:
                prev()
            prev = comp
        if prev is not None:
            prev()

    def build(self):
        nc, tr = self.nc, self.tr
        self._zsl = self._vst = self._pb = self._gsl = self._xsl = self._ssl = 0
        self._p1_j = self._p1_v = self._p2_c = self._r_c = self._f_j = 0
        tr.add('dve', lambda: nc.vector.memset(self.EPSB[:], EPS), writes=['EPSB'])
        self.setup()
        self.fence()
        for s in range(self.nseq):
            self.stage_in(s)
        self.fence()
        for s in range(self.nseq):
            for l in range(self.nl):
                self._p1_j = self._p1_v = self._p2_c = 0
                pr = self.prime(self.units_p1(s, l))
                self.stage_norm(s, lambda c, l=l: self.gvec(0, l, c))
                self.run_units(None, pr)
                self.fence()
                self.wslot = 0
                pr = self.prime(self.units_p2(s, l))
                self.stage_attn_a(s, l)
                self.fence()
                self.stage_attn_b(s, l)
                self.fence()
                self.run_units(None, pr)
                self.fence()
                for blk in range(2):
                    pr = self.prime(self.units_ffn(s, l, blk))
                    self.stage_norm(s, lambda c, l=l: self.gvec(1, l, c), t0=blk * 1024, nt=1024, ffn=True)
                    self.run_units(None, pr)
                self.fence()
            self.stage_out(s)
            self.fence()
        tr.finalize()
        return nc


def relayout(W, c0, ncols):
    K = W.shape[0]
    sub = W[:, c0:c0 + ncols].reshape(K // 128, 128, ncols // 128, 128)
    return np.ascontiguousarray(sub.transpose(2, 1, 0, 3)).reshape(ncols // 128, 128, K)


def host_consts(nl, norm_mix, norm_ffn, norm_final, gate_bias, diff_lambda, diff_subln):
    NF = 128 + 4 * 16 * nl + 16 + nl
    cf = np.zeros((128, NF), np.float32)
    cf[:, 0:128] = np.eye(128, dtype=np.float32)
    o = 128
    for arr in (norm_mix[:nl], norm_ffn[:nl]):
        cf[:, o:o + nl * 16] = arr.reshape(nl, 16, 128).transpose(2, 0, 1).reshape(128, nl * 16)
        o += nl * 16
    cf[:, o:o + nl * 32] = gate_bias[:nl].reshape(nl, 2, 16, 128).transpose(3, 0, 1, 2).reshape(128, nl * 32)
    o += nl * 32
    cf[:, o:o + 16] = norm_final.reshape(16, 128).T
    o += 16
    cf[:, o:o + nl] = diff_subln[:nl].T
    dl = np.ascontiguousarray(np.broadcast_to(diff_lambda[:nl].reshape(1, nl * 256), (128, nl * 256))).astype(np.float32)
    cb = np.zeros((128, 768), np.float32)
    cb[:, 0:128] = 1.0
    pa = np.zeros((128, 128), np.float32)
    for off in (0, 64):
        for i in range(8):
            pa[off + i + 8, off + i] = -1.0
            pa[off + i, off + i + 8] = 1.0
    pb = np.zeros((128, 128), np.float32)
    for i in range(16):
        pb[i + 16, i] = -1.0
        pb[i, i + 16] = 1.0
    cb[:, 128:256] = pa
    cb[:, 256:384] = pb
    kk = np.arange(128)[:, None]
    ql = np.arange(128)[None, :]
    cb[:, 384:512] = (kk >= ql)
    cb[:, 512:640] = (kk <= ql)
    cb[0:64, 640:768] = ((kk[0:64] + 64) >= ql)
    rope = np.zeros((4, 128, S), np.float32)
    rope[0] = 1.0
    rope[2] = 1.0
    pos = np.arange(S, dtype=np.float32)
    for ti, rot, offs in ((0, 16, (0, 64)), (2, 32, (0,))):
        inv = (np.float32(THETA) ** (-(np.arange(0, rot, 2, dtype=np.float32) / np.float32(rot)))).astype(np.float32)
        ang = (pos[:, None] * inv[None, :]).astype(np.float32)
        c, s_ = np.cos(ang).astype(np.float32).T, np.sin(ang).astype(np.float32).T
        half = rot // 2
        for off in offs:
            rope[ti, off:off + half] = c
            rope[ti, off + half:off + rot] = c
            rope[ti + 1, off:off + half] = s_
            rope[ti + 1, off + half:off + rot] = s_
    return cf, dl, cb, rope


def host_weights(nl, w_in, w_proj_a, w_proj_b, w_out, w_ffn_in, w_ffn_out):
    w_inF = np.empty((nl, NQK + NG, 128, 2048), np.float32)
    w_v = np.empty((nl, 5, 128, 8192), np.float32)
    w_pab = np.empty((nl, 16, 128, 1536), np.float32)
    w_o = np.empty((nl, 16, 128, 2048), np.float32)
    w_f1 = np.empty((nl, FC, 128, 4096), np.float32)
    w_f2 = np.empty((nl, 16, 128, FC * 128), np.float32)
    for l in range(nl):
        W = w_in[l]
        w_inF[l, 0:8] = relayout(W, 0, 1024)
        w_inF[l, 8:16] = relayout(W, 1024, 1024)
        w_inF[l, 16:28] = relayout(W, 3072, 1536)
        w_inF[l, 28:40] = relayout(W, 4608, 1536)
        w_inF[l, 40:72] = relayout(W, 7680, 4096)
        vcols = np.concatenate([W[:, 2048:3072], W[:, 6144:7680]], axis=1)
        w_v[l] = np.ascontiguousarray(vcols.reshape(16, 128, 5, 512).transpose(2, 1, 0, 3)).reshape(5, 128, 8192)
        w_pab[l, :, :, 0:1024] = relayout(w_proj_a[l], 0, 2048)
        w_pab[l, :, :, 1024:1536] = relayout(w_proj_b[l], 0, 2048)
        w_o[l] = relayout(w_out[l], 0, 2048)
        w_f1[l, :, :, 0:2048] = relayout(w_ffn_in[l], 0, FF)
        w_f1[l, :, :, 2048:4096] = relayout(w_ffn_in[l], FF, FF)
        w_f2[l] = relayout(w_ffn_out[l], 0, 2048)
    return dict(w_inF=w_inF, w_v=w_v, w_pab=w_pab, w_o=w_o, w_f1=w_f1, w_f2=w_f2)


_PROG_CACHE = {}


def run_cores(xs_per_core, nl, nseq, weights, consts):
    key = (nl, nseq)
    if key not in _PROG_CACHE:
        _PROG_CACHE[key] = Prog(nl, nseq).build()
    nc = _PROG_CACHE[key]
    cf, dl, cb, rope = consts
    in_maps = []
    for xc in xs_per_core:
        m = dict(x=np.ascontiguousarray(xc, dtype=np.float32), c_f32=cf, c_dlam=dl, c_bf=cb, c_rope=rope)
        m.update(weights)
        in_maps.append(m)
    res = run_bass_kernel_spmd(nc, in_maps, core_ids=list(range(len(in_maps))))
    return [r["y"] for r in res.results]


def kernel(x_prompt, x_sample, norm_mix, norm_ffn, w_in, gate_bias, diff_lambda, diff_subln,
           w_proj_a, w_proj_b, w_out, w_ffn_in, w_ffn_out, norm_final):
    f = lambda a: np.asarray(a, dtype=np.float32)
    x_prompt, x_sample = f(x_prompt), f(x_sample)
    weights = host_weights(NL, f(w_in), f(w_proj_a), f(w_proj_b), f(w_out), f(w_ffn_in), f(w_ffn_out))
    consts = host_consts(NL, f(norm_mix), f(norm_ffn), f(norm_final), f(gate_bias), f(diff_lambda), f(diff_subln))
    xs = []
    for c in range(4):
        xs.append(x_prompt[2 * c:2 * c + 2])
    for c in range(4):
        xs.append(np.stack([x_sample[c], x_sample[c]]))
    ys = run_cores(xs, NL, 2, weights, consts)
    y_prompt = np.concatenate([ys[c] for c in range(4)], axis=0)
    y_sample = np.stack([ys[4 + c][0] for c in range(4)], axis=0)
    return (y_prompt.astype(np.float32), y_sample.astype(np.float32))
```

```python
import math
import os
import numpy as np
import concourse.bass as bass
import concourse.mybir as mybir
from concourse.bass_utils import run_bass_kernel_spmd

F32 = mybir.dt.float32
BF16 = mybir.dt.bfloat16
AF = mybir.ActivationFunctionType
ALU = mybir.AluOpType

D = 2048
S = 2048
NL = 4
FF = 5632
KC = 16
FC = 44
EPS = 1e-6
FUSE_P2 = os.environ.get('FUSE_P2', '0') == '1'
FUSE_F2 = os.environ.get('FUSE_F2', '0') == '1'
NQK = 40
NG = 32
VW = 2560
B_DIL = (1, 4, 16)
THETA = 500000.0


def lam_init_for(l):
    return 0.8 - 0.6 * math.exp(-0.3 * l)


class Trk:
    def __init__(self, nc):
        self.nc = nc
        self.ops = []
        self.eng = {'pe': nc.tensor, 'act': nc.scalar, 'dve': nc.vector, 'pool': nc.gpsimd, 'sp': nc.sync}

    def add(self, eng, fn, reads=(), writes=(), dma=None):
        self.ops.append((eng, fn, tuple(reads), tuple(writes), dma))

    def fence(self, fn):
        self.ops.append(('sp', fn, ('__ALL__',), (), 'd_fence'))

    def finalize(self):
        nc = self.nc
        ops = self.ops
        n = len(ops)
        deps = [None] * n
        last_w = {}
        last_r = {}
        fence_i = None
        for i, (eng, fn, reads, writes, dma) in enumerate(ops):
            d = set()
            if reads == ('__ALL__',):
                d.update(last_w.values())
                for r in last_r.values():
                    d.update(r.values())
                if fence_i is not None:
                    d.add(fence_i)
                last_w = {}
                last_r = {}
                fence_i = i
                deps[i] = d
                continue
            if fence_i is not None:
                d.add(fence_i)
            for k in reads:
                w = last_w.get(k)
                if w is not None:
                    d.add(w)
            for k in writes:
                w = last_w.get(k)
                if w is not None:
                    d.add(w)
                r = last_r.get(k)
                if r:
                    d.update(r.values())
            for k in writes:
                last_w[k] = i
                last_r[k] = {}
            cls = ('dma', dma) if dma else eng
            for k in reads:
                lr = last_r.get(k)
                if lr is None:
                    lr = last_r[k] = {}
                lr[cls] = i
            d.discard(i)
            if eng == 'pe' and not dma:
                d = {j for j in d if not (ops[j][0] == 'pe' and not ops[j][4])}
            deps[i] = d
        needed = [False] * n
        for i in range(n):
            for j in deps[i]:
                needed[j] = True
        sems = {}

        def sem(name):
            if name not in sems:
                sems[name] = nc.alloc_semaphore(name=name)
            return sems[name]

        val = [0] * n
        semname = [None] * n
        cnt = {}
        for i, (eng, fn, reads, writes, dma) in enumerate(ops):
            if dma:
                cnt[dma] = cnt.get(dma, 0) + 16
                val[i] = cnt[dma]
                semname[i] = dma
            elif needed[i]:
                key = 'E_' + eng
                cnt[key] = cnt.get(key, 0) + 1
                val[i] = cnt[key]
                semname[i] = key
        waited = {e: {} for e in self.eng}
        issued = {}
        for i, (eng, fn, reads, writes, dma) in enumerate(ops):
            e = self.eng[eng]
            need = {}
            for j in deps[i]:
                sn = semname[j]
                if val[j] > need.get(sn, 0):
                    need[sn] = val[j]
            for sn, v in need.items():
                if waited[eng].get(sn, 0) >= v:
                    continue
                if not sn.startswith('E_'):
                    v = issued.get(sn, 0)
                e.wait_ge(sem(sn), v)
                waited[eng][sn] = v
            inst = fn()
            if dma:
                inst.then_inc(sem(dma), 16)
                issued[dma] = issued.get(dma, 0) + 16
            elif needed[i]:
                inst.then_inc(sem(semname[i]), 1)
        sp = self.eng['sp']
        for sn, v in cnt.items():
            if waited['sp'].get(sn, 0) < v:
                sp.wait_ge(sem(sn), v)
        self.nsems = len(sems)
        self.nops = n


class Prog:
    def __init__(self, nl=NL, nseq=2, dbg=False):
        self.nl = nl
        self.nseq = nseq
        nc = self.nc = bass.Bass("TRN2", target_bir_lowering=False)
        self.tr = Trk(nc)
        dt = nc.dram_tensor
        self.x_in = dt("x", [nseq, S, D], F32, kind="ExternalInput").ap()
        self.y_out = dt("y", [nseq, S, D], F32, kind="ExternalOutput").ap()
        self.w_inF = dt("w_inF", [nl, NQK + NG, 128, 2048], F32, kind="ExternalInput").ap()
        self.w_v = dt("w_v", [nl, 5, 128, 8192], F32, kind="ExternalInput").ap()
        self.w_pab = dt("w_pab", [nl, 16, 128, 1536], F32, kind="ExternalInput").ap()
        self.w_o = dt("w_o", [nl, 16, 128, 2048], F32, kind="ExternalInput").ap()
        self.w_f1 = dt("w_f1", [nl, FC, 128, 4096], F32, kind="ExternalInput").ap()
        self.w_f2 = dt("w_f2", [nl, 16, 128, FC * 128], F32, kind="ExternalInput").ap()
        self.NF = 128 + 4 * 16 * nl + 16 + nl
        self.c_f32 = dt("c_f32", [128, self.NF], F32, kind="ExternalInput").ap()
        self.c_dlam = dt("c_dlam", [128, nl * 256], F32, kind="ExternalInput").ap()
        self.c_bf = dt("c_bf", [128, 768], F32, kind="ExternalInput").ap()
        self.c_rope = dt("c_rope", [4, 128, S], F32, kind="ExternalInput").ap()
        self.xT = dt("xT", [nseq, D, S], F32, kind="Internal").ap()
        self.qkT = dt("qkT", [NQK * 128, S], BF16, kind="Internal").ap()
        self.gT = dt("gT", [NG * 128, S], BF16, kind="Internal").ap()
        self.vS = dt("vS", [S, VW], BF16, kind="Internal").ap()
        self._off = (nc.sbuf_base + 63) // 64 * 64
        self._top = nc.sbuf_top
        A = self.alloc
        self.XN = A("XN", [128, 16, S], BF16)
        self.W = A("W", [128, 2, 8192], BF16)
        self.XC = A("XC", [128, 2, 2048], F32)
        self.CF = A("CF", [128, self.NF], F32)
        self.CB = A("CB", [128, 768], BF16)
        self.LAM = A("LAM", [128, 2 * nl], F32)
        self.EPSB = A("EPSB", [128, 1], F32)
        self.FD = A("FD", [128, 16], F32)
        arena0 = self._off
        self.ROPE = A("ROPE", [128, 4, S], F32)
        self.RSTD = A("RSTD", [128, S], F32)
        self.SQ = A("SQ", [128, 2, S], BF16)
        self.ZB = A("ZB", [128, 2, 1024], BF16)
        self.T1 = A("T1", [128, 2, 1024], F32)
        self.T2 = A("T2", [128, 2, 1024], F32)
        self.OST = A("OST", [128, 2, 1024], BF16)
        self.VST = A("VST", [128, 2, 512], BF16)
        end1 = self._off
        self._off = arena0
        self.OA = A("OA", [128, 8, S], BF16)
        self.OB = A("OB", [128, 4, S], BF16)
        qt0 = self._off
        self.QT = A("QT", [128, 2, S], BF16)
        self.KT = A("KT", [128, 2, S], BF16)
        vt0 = self._off
        self.VT = A("VT", [128, 2, 16, 128], BF16)
        sub0 = self._off
        self.PT = A("PT", [128, 4, 512], BF16)
        self.EP = A("EP", [128, 6, 512], F32)
        self.SQA = A("SQA", [128, 512], BF16)
        self.PAB = A("PAB", [128, 2, 512], BF16)
        end2 = self._off
        self._off = vt0
        self.VO = A("VO", [128, 32, 128], BF16)
        self._off = sub0
        self.PB = A("PB", [128, 4, 128], BF16)
        self.OACC = A("OACC", [128, S], F32)
        self.ZACC = A("ZACC", [128, S], F32)
        end3 = self._off
        self._off = qt0
        self.GA = A("GA", [128, 2, 1024], BF16)
        self.GB = A("GB", [128, 2, 1024], BF16)
        self.M1 = A("M1", [128, 2, 1024], F32)
        self.M2 = A("M2", [128, 2, 1024], F32)
        self.SQR = A("SQR", [128, 2, 1024], BF16)
        end4 = self._off
        self._off = arena0
        self.ACTB = A("ACTB", [128, FC, 1024], BF16)
        self.SG = A("SG", [128, 1024], F32)
        end5 = self._off
        assert max(end1, end2, end3, end4, end5) <= self._top, (end1, end2, end3, end4, end5, self._top)
        self.PS = nc.alloc_psum_tensor("PS", [128, 8, 512], F32)
        self.wslot = 0
        self.pairs = [0, 1, 2, 3]
        self._quad = self._pi = self._si = 0
        self._pend_sq = None
        self._ob = 0
        self.phase = 0

    def alloc(self, name, shape, dtype):
        nbytes = int(np.prod(shape[1:])) * (4 if dtype == F32 else 2)
        t = self.nc.alloc_sbuf_tensor_at(name, shape, dtype, offset=self._off)
        self._off = (self._off + nbytes + 63) // 64 * 64
        return t

    def fence(self):
        nc = self.nc
        self.tr.fence(lambda: nc.sync.dma_start(out=self.FD[0:1, 0:16], in_=self.c_f32[0:1, 0:16]))

    def ak(self, name):
        return name

    def banks(self, n):
        P = self.pairs
        if os.environ.get('OLD_ALLOC', '1') == '1':
            b = self._ob
            if b + n > 8:
                b = 0
            self._ob = (b + n) % 8
            return list(range(b, b + n))
        if n == 4:
            quads = [q for q in (0, 1) if (2 * q in P and 2 * q + 1 in P)]
            q = quads[self._quad % len(quads)]
            self._quad += 1
            return [4 * q, 4 * q + 1, 4 * q + 2, 4 * q + 3]
        if n == 2:
            p = P[self._pi % len(P)]
            self._pi += 1
            return [2 * p, 2 * p + 1]
        idx = self._si % (2 * len(P))
        self._si += 1
        return [2 * P[idx // 2] + idx % 2]

    def ps(self, b0, nb=1):
        if nb == 1:
            return self.PS[:, b0, :]
        return self.PS[:, b0:b0 + nb, :]

    def mm(self, bank, out, lhsT, rhs, start, stop, reads, **kw):
        nc = self.nc
        self.tr.add('pe', lambda: nc.tensor.matmul(out, lhsT=lhsT, rhs=rhs, start=start, stop=stop, **kw),
                    reads=reads, writes=["PS%d" % bank])

    def dma(self, q, out, in_, sem, reads, writes):
        nc = self.nc
        e = {'sp': nc.sync, 'pool': nc.gpsimd}[q]
        self.tr.add(q, lambda: e.dma_start(out=out, in_=in_), reads=reads, writes=writes, dma=sem)

    def act(self, out, in_, func, reads, writes, bias=None, scale=None):
        nc = self.nc
        kw = {}
        if bias is not None:
            kw['bias'] = bias
        if scale is not None:
            kw['scale'] = scale
        self.tr.add('act', lambda: nc.scalar.activation(out=out, in_=in_, func=func, **kw), reads=reads, writes=writes)

    def tt(self, out, in0, in1, op, reads, writes):
        nc = self.nc
        self.tr.add('dve', lambda: nc.vector.tensor_tensor(out=out, in0=in0, in1=in1, op=op), reads=reads, writes=writes)

    def stt(self, out, in0, scalar, in1, op0, op1, reads, writes):
        nc = self.nc
        self.tr.add('dve', lambda: nc.vector.scalar_tensor_tensor(out=out, in0=in0, scalar=scalar, in1=in1, op0=op0, op1=op1),
                    reads=reads, writes=writes)

    def recip(self, out, in_, reads, writes):
        nc = self.nc
        self.tr.add('dve', lambda: nc.vector.reciprocal(out=out, in_=in_), reads=reads, writes=writes)

    def rfast(self, out, in_, reads, writes):
        self.recip(out, in_, reads, writes)

    def vcopy(self, out, in_, reads, writes):
        nc = self.nc
        self.tr.add('dve', lambda: nc.vector.tensor_copy(out=out, in_=in_), reads=reads, writes=writes)

    def ident(self):
        return self.CF[:, 0:128]

    def gvec(self, which, l, c):
        o = 128 + (which * self.nl + l) * 16 + c
        return self.CF[:, o:o + 1]

    def gbias(self, l, ab, c):
        o = 128 + 2 * 16 * self.nl + ((l * 2 + ab) * 16) + c
        return self.CF[:, o:o + 1]

    def gfinal(self, c):
        o = 128 + 4 * 16 * self.nl + c
        return self.CF[:, o:o + 1]

    def gsub_raw(self, l):
        o = 128 + 4 * 16 * self.nl + 16 + l
        return self.CF[:, o:o + 1]

    def setup(self):
        nc, tr, nl = self.nc, self.tr, self.nl
        self.dma('sp', self.CF[:], self.c_f32, 'd_cf', [], ['CF'])
        self.dma('pool', self.CB[:], self.c_bf, 'd_cb', [], ['CB'])
        DL = self.T1[:].rearrange("p a b -> p (a b)")[:, 0:nl * 256]
        PR = self.T2[:].rearrange("p a b -> p (a b)")[:, 0:nl * 128]
        SM = self.T2[:].rearrange("p a b -> p (a b)")[:, 1024:1024 + 2 * nl]
        EX = self.T2[:].rearrange("p a b -> p (a b)")[:, 1100:1100 + 2 * nl]
        self.dma('sp', DL, self.c_dlam, 'd_dl', [], ['DL'])
        dl4 = DL.rearrange("p (l f d) -> p l f d", f=4, d=64)
        pr3 = PR.rearrange("p (l t d) -> p l t d", t=2, d=64)
        for t in range(2):
            self.tt(pr3[:, :, t, :], dl4[:, :, 2 * t, :], dl4[:, :, 2 * t + 1, :], ALU.mult, ['DL'], ['PR%d' % t])
        tr.add('dve', lambda: nc.vector.reduce_sum(out=SM, in_=PR.rearrange("p (a d) -> p a d", d=64), axis=mybir.AxisListType.X),
               reads=['PR0', 'PR1'], writes=['SM'])
        self.act(EX, SM, AF.Exp, ['SM'], ['EX'])
        ex2 = EX.rearrange("p (l t) -> p l t", t=2)
        for l in range(nl):
            li = lam_init_for(l)
            o, i0, i1 = self.LAM[:, l:l + 1], ex2[:, l, 1:2], ex2[:, l, 0:1]
            tr.add('dve', (lambda o=o, i0=i0, i1=i1, li=li: nc.vector.scalar_tensor_tensor(
                out=o, in0=i0, scalar=-li, in1=i1, op0=ALU.add, op1=ALU.subtract)), reads=['EX'], writes=['LAM'])
            o2, g = self.LAM[:, nl + l:nl + l + 1], self.gsub_raw(l)
            tr.add('dve', (lambda o2=o2, g=g, li=li: nc.vector.tensor_scalar(
                out=o2, in0=g, scalar1=1.0 - li, scalar2=None, op0=ALU.mult)), reads=['CF'], writes=['LAM'])

    def stage_in(self, s):
        nc, tr = self.nc, self.tr
        for t in range(17):
            if t < 16:
                sl = t % 2
                self.dma('sp', self.XC[:, sl, :], self.x_in[s, t * 128:(t + 1) * 128, :], 'd_xc%d' % sl, [], ['XC%d' % sl])
            if t > 0:
                tt_ = t - 1
                sl = tt_ % 2
                bs = self.banks(4)
                for c in range(16):
                    b = bs[c // 4]
                    out = self.PS[:, b, (c % 4) * 128:(c % 4 + 1) * 128]
                    in_ = self.XC[:, sl, c * 128:(c + 1) * 128]
                    idn = self.ident()
                    tr.add('pe', (lambda out=out, in_=in_, idn=idn: nc.tensor.transpose(out, in_, idn)),
                           reads=['XC%d' % sl, 'CF'], writes=['PS%d' % b])
                st = tt_ % 2
                stg = self.T1[:, st, :] if False else None
                dst = self.ROPE[:, st, :]
                for q in range(4):
                    o = dst[:, q * 512:(q + 1) * 512]
                    i_ = self.PS[:, bs[q], :]
                    if q % 2 == 0:
                        self.act(o, i_, AF.Copy, ['PS%d' % bs[q]], ['ST0_%d' % st])
                    else:
                        self.vcopy(o, i_, ['PS%d' % bs[q]], ['ST0_%d' % st])
                dr = self.xT[s].rearrange("(c p) t -> p c t", p=128)[:, :, tt_ * 128:(tt_ + 1) * 128]
                self.dma('sp', dr, dst.rearrange("p (c t) -> p c t", t=128), 'd_st0_%d' % st, ['ST0_%d' % st], ['xT%d' % c for c in range(16)])

    def stage_norm(self, s, gfn, t0=0, nt=S, ffn=False, acc=None):
        nc, tr = self.nc, self.tr
        nb = nt // 512
        bs = acc if acc is not None else self.banks(nb)
        xsrc = self.xT[s].rearrange("(c p) t -> c p t", p=128)
        if ffn:
            rstd, rk = self.SG[:, 0:nt], 'SG'
            sq = [self.XC[:, i, 1024:2048].bitcast(BF16)[:, 0:nt] for i in range(2)]
            sqk = ['XR0', 'XR1']
        else:
            rstd, rk = self.RSTD[:, 0:nt], 'RSTD'
            sq = [self.SQ[:, i, 0:nt] for i in range(2)]
            sqk = ['SQ0', 'SQ1']
        for c in range(17 if acc is None else 0):
            if c < 16:
                sl = c % 2
                self.dma('sp', self.XC[:, sl, 0:nt], xsrc[c, :, t0:t0 + nt], 'd_xc%d' % sl, ['xT%d' % c], ['XC%d' % sl])
            if c > 0:
                cc = c - 1
                sl = cc % 2
                self.act(sq[sl], self.XC[:, sl, 0:nt], AF.Square, ['XC%d' % sl], [sqk[sl]])
                for b in range(nb):
                    self.mm(bs[b], self.PS[:, bs[b], :], self.CB[:, 0:128], sq[sl][:, b * 512:(b + 1) * 512],
                            cc == 0, cc == 15, [sqk[sl], 'CB'])
        psk = ['PS%d' % b for b in bs]
        rps = self.PS[:, bs[0]:bs[0] + nb, :]
        self.act(rstd.rearrange("p (b f) -> p b f", f=512), rps, AF.Sqrt, psk, [rk], bias=self.epsb(), scale=1.0 / D)
        self.rfast(rps, rstd.rearrange("p (b f) -> p b f", f=512), [rk], psk)
        for c in range(17):
            if c < 16:
                sl = c % 2
                self.dma('sp', self.XC[:, sl, 0:nt], xsrc[c, :, t0:t0 + nt], 'd_xc%d' % sl, ['xT%d' % c], ['XC%d' % sl])
            if c > 0:
                cc = c - 1
                sl = cc % 2
                self.stt(self.XN[:, cc, 0:nt].rearrange("p (b f) -> p b f", f=512), self.XC[:, sl, 0:nt].rearrange("p (b f) -> p b f", f=512),
                         gfn(cc), rps, ALU.mult, ALU.mult, ['XC%d' % sl, 'CF'] + psk, ['XN%d' % cc])

    def epsb(self):
        return self.EPSB[:, 0:1]

    def wgroups(self, wd, nch, E):
        G = max(1, 8192 // E)
        R = E
        while R > 2048:
            R //= 2
        if E == FC * 128:
            R = 1408
        assert E % R == 0
        for g0 in range(0, nch, G):
            n = min(G, nch - g0)
            slot = self.wslot
            self.wslot ^= 1

            def load(slot=slot, g0=g0, n=n):
                out = self.W[:, slot, 0:n * E].rearrange("p (n a r) -> p n a r", n=n, r=R)
                in_ = wd[g0:g0 + n].rearrange("n p (a r) -> p n a r", r=R)
                self.dma('pool', out, in_, 'd_w%d' % slot, [], ['W%d' % slot])
            yield load, [(slot, i * E) for i in range(n)]

    def wtile(self, slot, off, kc, width=128, kstride=128):
        o = off + kc * kstride
        return self.W[:, slot, o:o + width]

    def units_p1(self, s, l):
        nc, tr = self.nc, self.tr
        def loadrope():
            self.dma('sp', self.ROPE[:].rearrange("p a t -> p a t"), self.c_rope.rearrange("a p t -> p a t"), 'd_rope', [], ['ROPE'])
        first = [True]
        pend = [None]
        for load, tiles in self.wgroups(self.w_inF[l], NQK + NG, 2048):
            for gi, (slot, off) in enumerate(tiles):
                j = self._p1_j
                self._p1_j += 1

                def comp(j=j, slot=slot, off=off):
                    if j == 0:
                        loadrope()
                    if j < NQK:
                        self.p1_qk(s, l, j, slot, off, pend)
                    else:
                        if pend[0] is not None:
                            pend[0]()
                            pend[0] = None
                        self.p1_gate(s, l, j - NQK, slot, off)
                yield (load if gi == 0 else None), comp
        for load, tiles in self.wgroups(self.w_v[l], 5, 8192):
            for gi, (slot, off) in enumerate(tiles):
                vb = self._p1_v
                self._p1_v += 1

                def compv(vb=vb, slot=slot, off=off):
                    for t in range(16):
                        b = self.banks(1)[0]
                        for kc in range(16):
                            self.mm(b, self.PS[:, b, :], self.XN[:, kc, t * 128:(t + 1) * 128], self.W[:, slot, off + kc * 512:off + (kc + 1) * 512],
                                    kc == 0, kc == 15, ['XN%d' % kc, 'W%d' % slot])
                        st = self._vst
                        self._vst ^= 1
                        if t % 2 == 0:
                            self.act(self.VST[:, st, :], self.PS[:, b, :], AF.Copy, ['PS%d' % b], ['VST%d' % st])
                        else:
                            self.vcopy(self.VST[:, st, :], self.PS[:, b, :], ['PS%d' % b], ['VST%d' % st])
                        self.dma('sp', self.vS[t * 128:(t + 1) * 128, vb * 512:(vb + 1) * 512], self.VST[:, st, :], 'd_vst%d' % st,
                                 ['VST%d' % st], ['vS'])
                yield (load if gi == 0 else None), compv

    def p1_qk(self, s, l, j, slot, off, pend):
        nc, tr = self.nc, self.tr
        isB = j >= 16
        ct = 2 if isB else 0
        perm = self.CB[:, 256:384] if isB else self.CB[:, 128:256]
        grp = None
        if isB:
            jj = (j - 16) % 12
            grp = jj // 4
        for hf in range(2):
            zb_ = self.banks(2)
            for kc in range(16):
                for tb in range(2):
                    self.mm(zb_[tb], self.PS[:, zb_[tb], :], self.wtile(slot, off, kc), self.XN[:, kc, hf * 1024 + tb * 512:hf * 1024 + (tb + 1) * 512],
                            kc == 0, kc == 15, ['XN%d' % kc, 'W%d' % slot])
            if pend[0] is not None:
                pend[0]()
                pend[0] = None
            sl = self._zsl
            self._zsl ^= 1
            zps = self.PS[:, zb_[0]:zb_[0] + 2, :]
            zkeys = ['PS%d' % b for b in zb_]
            tsl = slice(hf * 1024, (hf + 1) * 1024)
            v3 = lambda ap: ap.rearrange("p (b f) -> p b f", f=512)
            self.act(v3(self.ZB[:, sl, :]), zps, AF.Copy, zkeys, ['ZB%d' % sl])
            self.tt(v3(self.T1[:, sl, :]), zps, v3(self.ROPE[:, ct, tsl]), ALU.mult, zkeys + ['ROPE'], ['T1_%d' % sl])

            def rest(j=j, hf=hf, sl=sl, perm=perm, ct=ct, grp=grp, tsl=tsl, v3=v3):
                pb_ = self.banks(2)
                for tb in range(2):
                    self.mm(pb_[tb], self.PS[:, pb_[tb], :], perm, self.ZB[:, sl, tb * 512:(tb + 1) * 512], True, True, ['ZB%d' % sl, 'CB'])
                pps = self.PS[:, pb_[0]:pb_[0] + 2, :]
                self.tt(v3(self.T2[:, sl, :]), pps, v3(self.ROPE[:, ct + 1, tsl]), ALU.mult, ['PS%d' % b for b in pb_] + ['ROPE'], ['T2_%d' % sl])
                o = self.OST[:, sl, :]
                if grp is None or B_DIL[grp] == 1:
                    oo = o
                    dst = self.qkT[j * 128:(j + 1) * 128, hf * 1024:(hf + 1) * 1024]
                    self.tt(oo, self.T1[:, sl, :], self.T2[:, sl, :], ALU.add, ['T1_%d' % sl, 'T2_%d' % sl], ['OST%d' % sl])
                    self.dma('sp', dst, o, 'd_ost%d' % sl, ['OST%d' % sl], ['qkT'])
                else:
                    d = B_DIL[grp]
                    oo = o.rearrange("p (q m) -> p m q", q=d)
                    i1 = self.T1[:, sl, :].rearrange("p (m q) -> p m q", q=d)
                    i2 = self.T2[:, sl, :].rearrange("p (m q) -> p m q", q=d)
                    self.tt(oo, i1, i2, ALU.add, ['T1_%d' % sl, 'T2_%d' % sl], ['OST%d' % sl])
                    nsub = S // d
                    hl = 1024 // d
                    dst = self.qkT[j * 128:(j + 1) * 128, :].rearrange("r (q m) -> r q m", q=d)[:, :, hf * hl:(hf + 1) * hl]
                    self.dma('sp', dst, o.rearrange("p (q m) -> p q m", q=d), 'd_ost%d' % sl, ['OST%d' % sl], ['qkT'])
            pend[0] = rest

    def p1_gate(self, s, l, g, slot, off):
        ab, c = g // 16, g % 16
        for hf in range(2):
            zb_ = self.banks(2)
            for kc in range(16):
                for tb in range(2):
                    self.mm(zb_[tb], self.PS[:, zb_[tb], :], self.wtile(slot, off, kc), self.XN[:, kc, hf * 1024 + tb * 512:hf * 1024 + (tb + 1) * 512],
                            kc == 0, kc == 15, ['XN%d' % kc, 'W%d' % slot])
            sl = self._zsl
            self._zsl ^= 1
            self.act(self.OST[:, sl, :].rearrange("p (b f) -> p b f", f=512), self.PS[:, zb_[0]:zb_[0] + 2, :], AF.Sigmoid,
                     ['PS%d' % b for b in zb_] + ['CF'], ['OST%d' % sl], bias=self.gbias(l, ab, c))
            self.dma('sp', self.gT[g * 128:(g + 1) * 128, hf * 1024:(hf + 1) * 1024], self.OST[:, sl, :], 'd_ost%d' % sl, ['OST%d' % sl], ['gT'])

    def stage_attn_a(self, s, l):
        nc, tr, nl = self.nc, self.tr, self.nl
        scale = 64 ** -0.5
        neglam = self.LAM[:, l:l + 1]
        gsub = self.LAM[:, nl + l:nl + l + 1]
        ones = self.CB[:, 0:128]
        E = self.EP
        OC = [E[:, 0, :], E[:, 1, :]]
        RC = [E[:, 2, :], E[:, 3, :]]
        SS = self.OB[:, 0:2, :].rearrange("p a t -> p (a t)").bitcast(F32)

        def loads(h):
            sl = h % 2
            self.dma('sp', self.QT[:, sl, :], self.qkT[h * 128:(h + 1) * 128, :], 'd_qt%d' % sl, ['qkT'], ['QT%d' % sl])
            self.dma('sp', self.KT[:, sl, :], self.qkT[(8 + h) * 128:(9 + h) * 128, :], 'd_kt%d' % sl, ['qkT'], ['KT%d' % sl])
            self.dma('sp', self.VT[:, sl, :, :], self.vS[:, h * 128:(h + 1) * 128].rearrange("(t p) e -> p t e", p=128), 'd_vt%d' % sl,
                     ['vS'], ['VT%d' % sl])
        loads(0)
        pend = [None]
        for h in range(8):
            if h + 1 < 8:
                loads(h + 1)
            sl = h % 2
            qk = ['QT%d' % sl, 'KT%d' % sl]
            for qb in range(4):
                qs = slice(qb * 512, (qb + 1) * 512)

                def s_step(kt):
                    sb = 2 * (kt % 2)
                    for c in range(2):
                        self.mm(sb + c, self.PS[:, sb + c, :], self.KT[64 * c:64 * c + 64, sl, kt * 128:(kt + 1) * 128],
                                self.QT[64 * c:64 * c + 64, sl, qs], True, True, qk)
                    self.act(self.PT[:, sb:sb + 2, :], self.PS[:, sb:sb + 2, :], AF.Exp, ['PS%d' % sb, 'PS%d' % (sb + 1)],
                             ['PT%d' % sb, 'PT%d' % (sb + 1)], scale=scale)

                def av_step(kt):
                    for c in range(2):
                        p = (kt % 2) * 2 + c
                        self.mm(4 + 2 * c, self.PS[:, 4 + 2 * c, :], self.VT[:, sl, kt, :], self.PT[:, p, :], kt == 0, kt == 15, ['VT%d' % sl, 'PT%d' % p])
                        self.mm(5 + 2 * c, self.PS[:, 5 + 2 * c, :], ones, self.PT[:, p, :], kt == 0, kt == 15, ['CB', 'PT%d' % p])
                s_step(0)
                for kt in range(16):
                    if kt + 1 < 16:
                        s_step(kt + 1)
                    av_step(kt)
                    if kt == 4 and pend[0] is not None:
                        pend[0]()
                        pend[0] = None
                for c in range(2):
                    self.vcopy(RC[c], self.PS[:, 5 + 2 * c, :], ['PS%d' % (5 + 2 * c)], ['RC%d' % c])
                    self.act(OC[c], self.PS[:, 4 + 2 * c, :], AF.Copy, ['PS%d' % (4 + 2 * c)], ['OC%d' % c])
                for c in range(2):
                    self.recip(RC[c], RC[c], ['RC%d' % c], ['RC%d' % c])
                    self.tt(OC[c], OC[c], RC[c], ALU.mult, ['OC%d' % c, 'RC%d' % c], ['OC%d' % c])
                self.stt(OC[0], OC[1], neglam, OC[0], ALU.mult, ALU.add, ['OC0', 'OC1', 'LAM'], ['OC0'])
                self.tt(self.SQA[:, :], OC[0], OC[0], ALU.mult, ['OC0'], ['SQA'])

                def part2(h=h, qs=qs, qb=qb):
                    self.mm(0, self.PS[:, 0, :], ones, self.SQA[:, :], True, True, ['SQA', 'CB'])
                    self.vcopy(SS[:, qs], self.PS[:, 0, :], ['PS0'], ['SS'])
                    oah = self.OA[:, h, qs]
                    tr.add('dve', (lambda oah=oah: nc.vector.tensor_scalar(out=oah, in0=OC[0], scalar1=gsub, scalar2=None, op0=ALU.mult)),
                           reads=['OC0', 'LAM'], writes=['OA%d' % h])
                    if qb == 3:
                        self.act(SS[:, :], SS[:, :], AF.Sqrt, ['SS'], ['SS'], bias=self.epsb(), scale=1.0 / 128)
                        self.recip(SS[:, :], SS[:, :], ['SS'], ['SS'])
                        self.tt(self.OA[:, h, :], self.OA[:, h, :], SS[:, :], ALU.mult, ['OA%d' % h, 'SS'], ['OA%d' % h])
                pend[0] = part2
        pend[0]()

    def stage_attn_b(self, s, l):
        nc, tr = self.nc, self.tr
        scale = 128 ** -0.5
        ones = self.CB[:, 0:128]
        mHi, mLo, mFi = self.CB[:, 384:512], self.CB[:, 512:640], self.CB[:, 640:768]
        mBoth = self.CB[:, 384:640]
        VOs = [self.VO, self.W[:, 1, 0:4096].rearrange("p (t e) -> p t e", e=128)]
        combos = [(h, g) for h in range(4) for g in range(3)]

        def loads(idx):
            h, g = combos[idx]
            d = B_DIL[g]
            nsub = S // d
            J = nsub // 128
            jq = 16 + g * 4 + h
            jk = 28 + g * 4 + h
            sl = idx % 2
            self.dma('sp', self.QT[:, sl, :], self.qkT[jq * 128:(jq + 1) * 128, :], 'd_qt%d' % sl, ['qkT'], ['QT%d' % sl])
            self.dma('sp', self.KT[:, sl, :], self.qkT[jk * 128:(jk + 1) * 128, :], 'd_kt%d' % sl, ['qkT'], ['KT%d' % sl])
            vcol = 1024 + (g * 4 + h) * 128
            vsrc = self.vS[:, vcol:vcol + 128]
            VO = VOs[sl]
            nt_ = J + 1
            vk, vsem = 'VO%d' % sl, 'd_vo%d' % sl
            for p in range(d):
                if J > 1:
                    src = vsrc[(64 * d + p):(64 * d + p) + ((J - 1) * 128 - 1) * d + 1:d, :].rearrange("(j k) e -> k j e", k=128)
                    self.dma('sp', VO[:, p * nt_ + 1:p * nt_ + J, :], src, vsem, ['vS'], [vk])
            srcf = vsrc[0:64 * d, :].rearrange("(k p) e -> k p e", p=d)
            self.dma('sp', VO[0:64, 0:(d - 1) * nt_ + 1:nt_, :], srcf, vsem, ['vS'], [vk])
            srcl = vsrc[(nsub - 64) * d:nsub * d, :].rearrange("(k p) e -> k p e", p=d)
            self.dma('sp', VO[0:64, J:J + (d - 1) * nt_ + 1:nt_, :], srcl, vsem, ['vS'], [vk])

        loads(0)
        for idx, (h, g) in enumerate(combos):
            if idx + 1 < len(combos):
                loads(idx + 1)
            d = B_DIL[g]
            nsub = S // d
            J = nsub // 128
            sl = idx % 2
            VO = VOs[sl]
            vk = 'VO%d' % sl
            nt_ = J + 1
            qk = ['QT%d' % sl, 'KT%d' % sl]
            blocks = [(p, u) for p in range(d) for u in range(J)]
            info = {}

            def st1(i):
                p, u = blocks[i]
                q0 = p * nsub + u * 128
                sb = self.banks(1)[0]
                tiles = []
                for ti, j in enumerate((u, u + 1)):
                    if j == 0:
                        k0, M, mask = p * nsub, 64, mFi[0:64, :]
                    elif j == J:
                        k0, M, mask = p * nsub + nsub - 64, 64, mLo[0:64, :]
                    else:
                        k0, M, mask = p * nsub + 128 * j - 64, 128, (mHi if ti == 0 else mLo)
                    tiles.append((j, k0, M, mask))
                for ti, (j, k0, M, mask) in enumerate(tiles):
                    self.mm(sb, self.PS[0:M, sb, ti * 128:(ti + 1) * 128], self.KT[:, sl, k0:k0 + M], self.QT[:, sl, q0:q0 + 128], True, True, qk)
                pp = self._pb
                self._pb = (self._pb + 2) % 4
                pks = ['PB%d' % pp, 'PB%d' % (pp + 1)]
                if tiles[0][2] == 128 and tiles[1][2] == 128:
                    pb2 = self.PB[:, pp:pp + 2, :]
                    self.act(pb2, self.PS[:, sb, 0:256].rearrange("p (a b) -> p a b", b=128), AF.Exp, ['PS%d' % sb], pks, scale=scale)
                    self.tt(pb2, pb2, mBoth.rearrange("p (a b) -> p a b", b=128), ALU.mult, pks + ['CB'], pks)
                else:
                    for ti, (j, k0, M, mask) in enumerate(tiles):
                        self.act(self.PB[0:M, pp + ti, :], self.PS[0:M, sb, ti * 128:(ti + 1) * 128], AF.Exp, ['PS%d' % sb], [pks[ti]], scale=scale)
                        self.tt(self.PB[0:M, pp + ti, :], self.PB[0:M, pp + ti, :], mask, ALU.mult, [pks[ti], 'CB'], [pks[ti]])
                info[i] = (tiles, pp)

            def st2(i):
                p, u = blocks[i]
                tiles, pp = info.pop(i)
                ob = self.banks(1)[0]
                for ti, (j, k0, M, mask) in enumerate(tiles):
                    self.mm(ob, self.PS[:, ob, 0:128], VO[0:M, p * nt_ + j, :], self.PB[0:M, pp + ti, :], ti == 0, ti == 1, [vk, 'PB%d' % (pp + ti)])
                for ti, (j, k0, M, mask) in enumerate(tiles):
                    self.mm(ob, self.PS[:, ob, 128:256], ones[0:M, :], self.PB[0:M, pp + ti, :], ti == 0, ti == 1, ['CB', 'PB%d' % (pp + ti)])
                st = (u * 128) * d + p
                oa = self.OACC[:, st:st + 127 * d + 1:d]
                za = self.ZACC[:, st:st + 127 * d + 1:d]
                if g == 0:
                    self.vcopy(oa, self.PS[:, ob, 0:128], ['PS%d' % ob], ['OACC'])
                    self.vcopy(za, self.PS[:, ob, 128:256], ['PS%d' % ob], ['ZACC'])
                else:
                    self.tt(oa, self.PS[:, ob, 0:128], oa, ALU.add, ['PS%d' % ob, 'OACC'], ['OACC'])
                    self.tt(za, self.PS[:, ob, 128:256], za, ALU.add, ['PS%d' % ob, 'ZACC'], ['ZACC'])

            st1(0)
            for i in range(len(blocks)):
                if i + 1 < len(blocks):
                    st1(i + 1)
                st2(i)
            if g == 2:
                self.rfast(self.ZACC[:, :], self.ZACC[:, :], ['ZACC'], ['ZACC'])
                self.tt(self.OB[:, h, :], self.OACC[:, :], self.ZACC[:, :], ALU.mult, ['OACC', 'ZACC'], ['OB%d' % h])

    def units_p2(self, s, l):
        for load, tiles in self.wgroups(self.w_pab[l], 16, 1536):
            for gi, (slot, off) in enumerate(tiles):
                c = self._p2_c
                self._p2_c += 1

                def comp(c=c, slot=slot, off=off):
                    for hf in range(2):
                        sl = self._gsl
                        self._gsl ^= 1
                        ts_ = slice(hf * 1024, (hf + 1) * 1024)
                        self.dma('sp', self.GA[:, sl, :], self.gT[c * 128:(c + 1) * 128, ts_], 'd_ga%d' % sl, ['gT'], ['GA%d' % sl])
                        self.dma('sp', self.GB[:, sl, :], self.gT[(16 + c) * 128:(17 + c) * 128, ts_], 'd_gb%d' % sl, ['gT'], ['GB%d' % sl])
                        ya = self.banks(2)
                        for kc in range(8):
                            for tb in range(2):
                                self.mm(ya[tb], self.PS[:, ya[tb], :], self.wtile(slot, off, kc), self.OA[:, kc, hf * 1024 + tb * 512:hf * 1024 + (tb + 1) * 512],
                                        kc == 0, kc == 7, ['OA%d' % kc, 'W%d' % slot])
                        yb = self.banks(2)
                        for kc in range(4):
                            for tb in range(2):
                                self.mm(yb[tb], self.PS[:, yb[tb], :], self.wtile(slot, off, 8 + kc), self.OB[:, kc, hf * 1024 + tb * 512:hf * 1024 + (tb + 1) * 512],
                                        kc == 0, kc == 3, ['OB%d' % kc, 'W%d' % slot])
                        v3 = lambda ap: ap.rearrange("p (b f) -> p b f", f=512)
                        self.tt(v3(self.M1[:, sl, :]), self.PS[:, ya[0]:ya[0] + 2, :], v3(self.GA[:, sl, :]), ALU.mult,
                                ['PS%d' % b for b in ya] + ['GA%d' % sl], ['M1_%d' % sl])
                        self.tt(v3(self.M2[:, sl, :]), self.PS[:, yb[0]:yb[0] + 2, :], v3(self.GB[:, sl, :]), ALU.mult,
                                ['PS%d' % b for b in yb] + ['GB%d' % sl], ['M2_%d' % sl])
                        self.tt(self.XN[:, c, ts_], self.M1[:, sl, :], self.M2[:, sl, :], ALU.add, ['M1_%d' % sl, 'M2_%d' % sl], ['XN%d' % c])
                yield (load if gi == 0 else None), comp
        if FUSE_P2:
            self.pairs = [0, 1]
        yield from self.units_resid(s, self.w_o[l], 2048, 16, self.XN, 'XN', 0, S, acc=([[4, 5], [6, 7]] if FUSE_P2 else None),
                                    sqb=[self.SQR[:, 0, :], self.SQR[:, 1, :]])

    def units_resid(self, s, wd, E, Kc, act, actkey, t0, nt, acc=None, sqb=None):
        xsrc = self.xT[s].rearrange("(c p) t -> c p t", p=128)
        nh = nt // 1024
        ones = self.CB[:, 0:128]
        for load, tiles in self.wgroups(wd, 16, E):
            for gi, (slot, off) in enumerate(tiles):
                c = self._r_c % 16
                self._r_c += 1

                def comp(c=c, slot=slot, off=off):
                    for hf in range(nh):
                        sl = self._xsl
                        self._xsl ^= 1
                        ts_ = slice(t0 + hf * 1024, t0 + (hf + 1) * 1024)
                        self.dma('sp', self.XC[:, sl, 0:1024], xsrc[c, :, ts_], 'd_xc%d' % sl, ['xT%d' % c], ['XC%d' % sl])
                        zb_ = self.banks(2)
                        for kc in range(Kc):
                            for tb in range(2):
                                self.mm(zb_[tb], self.PS[:, zb_[tb], :], self.wtile(slot, off, kc), act[:, kc, hf * 1024 + tb * 512:hf * 1024 + (tb + 1) * 512],
                                        kc == 0, kc == Kc - 1, ['%s%d' % (actkey, kc), 'W%d' % slot])
                        if self._pend_sq is not None:
                            self._pend_sq()
                            self._pend_sq = None
                        self.tt(self.XC[:, sl, 1024:2048].rearrange("p (b f) -> p b f", f=512), self.PS[:, zb_[0]:zb_[0] + 2, :],
                                self.XC[:, sl, 0:1024].rearrange("p (b f) -> p b f", f=512), ALU.add,
                                ['PS%d' % b for b in zb_] + ['XC%d' % sl], ['XR%d' % sl])
                        self.dma('sp', xsrc[c, :, ts_], self.XC[:, sl, 1024:2048], 'd_xr%d' % sl, ['XR%d' % sl], ['xT%d' % c])
                        if acc is not None:
                            self.act(sqb[sl], self.XC[:, sl, 1024:2048], AF.Square, ['XR%d' % sl], ['SQR%d' % sl])

                            def f(c=c, hf=hf, sl=sl):
                                for tb in range(2):
                                    b = acc[hf][tb]
                                    self.mm(b, self.PS[:, b, :], ones, sqb[sl][:, tb * 512:(tb + 1) * 512], c == 0, c == 15, ['SQR%d' % sl, 'CB'])
                            if c == 15 and hf == nh - 1:
                                f()
                            else:
                                self._pend_sq = f
                yield (load if gi == 0 else None), comp

    def units_ffn(self, s, l, blk):
        t0 = blk * 1024
        for load, tiles in self.wgroups(self.w_f1[l], FC, 4096):
            for gi, (slot, off) in enumerate(tiles):
                j = self._f_j % FC
                self._f_j += 1

                def comp(j=j, slot=slot, off=off):
                    gb_ = self.banks(2)
                    for kc in range(16):
                        for tb in range(2):
                            self.mm(gb_[tb], self.PS[:, gb_[tb], :], self.wtile(slot, off, kc), self.XN[:, kc, tb * 512:(tb + 1) * 512],
                                    kc == 0, kc == 15, ['XN%d' % kc, 'W%d' % slot])
                    ub_ = self.banks(2)
                    for kc in range(16):
                        for tb in range(2):
                            self.mm(ub_[tb], self.PS[:, ub_[tb], :], self.wtile(slot, off + 2048, kc), self.XN[:, kc, tb * 512:(tb + 1) * 512],
                                    kc == 0, kc == 15, ['XN%d' % kc, 'W%d' % slot])
                    sg = self.SG[:, :].rearrange("p (b f) -> p b f", f=512)
                    self.act(sg, self.PS[:, gb_[0]:gb_[0] + 2, :], AF.Silu, ['PS%d' % b for b in gb_], ['SG'])
                    self.tt(self.ACTB[:, j, :].rearrange("p (b f) -> p b f", f=512), self.PS[:, ub_[0]:ub_[0] + 2, :], sg, ALU.mult,
                            ['PS%d' % b for b in ub_] + ['SG'], ['ACTB%d' % j])
                yield (load if gi == 0 else None), comp
        if FUSE_F2:
            self.pairs = [0, 1]
        yield from self.units_resid(s, self.w_f2[l], FC * 128, FC, self.ACTB, 'ACTB', t0, 1024, acc=([[4 + 2 * blk, 5 + 2 * blk]] if FUSE_F2 else None),
                                    sqb=[self.XN[:, 14, 1024:2048], self.XN[:, 15, 1024:2048]])

    def stage_out(self, s, acc=None):
        nc, tr = self.nc, self.tr
        xsrc = self.xT[s].rearrange("(c p) t -> c p t", p=128)
        bs = acc if acc is not None else self.banks(4)
        for c in range(17 if acc is None else 0):
            if c < 16:
                sl = c % 2
                self.dma('sp', self.XC[:, sl, :], xsrc[c], 'd_xc%d' % sl, ['xT%d' % c], ['XC%d' % sl])
            if c > 0:
                cc = c - 1
                sl = cc % 2
                self.act(self.SQ[:, sl, :], self.XC[:, sl, :], AF.Square, ['XC%d' % sl], ['SQ%d' % sl])
                for b in range(4):
                    self.mm(bs[b], self.PS[:, bs[b], :], self.CB[:, 0:128], self.SQ[:, sl, b * 512:(b + 1) * 512], cc == 0, cc == 15, ['SQ%d' % sl, 'CB'])
        self.act(self.RSTD[:].rearrange("p (b f) -> p b f", f=512), self.PS[:, bs[0]:bs[0] + 4, :], AF.Sqrt,
                 ['PS%d' % b for b in bs], ['RSTD'], bias=self.epsb(), scale=1.0 / D)
        self.recip(self.RSTD[:], self.RSTD[:], ['RSTD'], ['RSTD'])
        YN = self.ROPE
        for c in range(17):
            if c < 16:
                sl = c % 2
                self.dma('sp', self.XC[:, sl, :], xsrc[c], 'd_xc%d' % sl, ['xT%d' % c], ['XC%d' % sl])
            if c > 0:
                cc = c - 1
                sl = cc % 2
                self.stt(YN[:, sl, :], self.XC[:, sl, :], self.gfinal(cc), self.RSTD[:], ALU.mult, ALU.mult, ['XC%d' % sl, 'RSTD', 'CF'], ['YN%d' % sl])
                tb_ = self.banks(4)
                for t in range(16):
                    b = tb_[t // 4]
                    out = self.PS[:, b, (t % 4) * 128:(t % 4 + 1) * 128]
                    in_ = YN[:, sl, t * 128:(t + 1) * 128]
                    idn = self.ident()
                    tr.add('pe', (lambda out=out, in_=in_, idn=idn: nc.tensor.transpose(out, in_, idn)), reads=['YN%d' % sl, 'CF'], writes=['PS%d' % b])
                dst = YN[:, 2 + sl, :]
                for q in range(4):
                    o = dst[:, q * 512:(q + 1) * 512]
                    if q % 2 == 0:
                        self.act(o, self.PS[:, tb_[q], :], AF.Copy, ['PS%d' % tb_[q]], ['YT%d' % sl])
                    else:
                        self.vcopy(o, self.PS[:, tb_[q], :], ['PS%d' % tb_[q]], ['YT%d' % sl])
                dr = self.y_out[s].rearrange("(t p) f -> p t f", p=128)[:, :, cc * 128:(cc + 1) * 128]
                self.dma('sp', dr, dst.rearrange("p (t f) -> p t f", f=128), 'd_yt%d' % sl, ['YT%d' % sl], ['y'])

    def prime(self, gen):
        load, comp = next(gen)
        if load is not None:
            load()
        return comp, gen

    def run_units(self, gen, primed=None):
        prev = None
        if primed is not None:
            prev, gen = primed
        for load, comp in (gen if gen is not None else ()):
            if load is not None:
                load()
            if prev is not None:
                prev()
            prev = comp
        if prev is not None:
            prev()

    def build(self):
        nc, tr = self.nc, self.tr
        self._zsl = self._vst = self._pb = self._gsl = self._xsl = self._ssl = 0
        self._p1_j = self._p1_v = self._p2_c = self._r_c = self._f_j = 0
        tr.add('dve', lambda: nc.vector.memset(self.EPSB[:], EPS), writes=['EPSB'])
        self.setup()
        self.fence()
        for s in range(self.nseq):
            self.stage_in(s)
        self.fence()
        for s in range(self.nseq):
            for l in range(self.nl):
                self._p1_j = self._p1_v = self._p2_c = 0
                pr = self.prime(self.units_p1(s, l))
                if l == 0 or not FUSE_F2:
                    self.pairs = [0, 1, 2, 3]
                    self.stage_norm(s, lambda c, l=l: self.gvec(0, l, c))
                else:
                    self.stage_norm(s, lambda c, l=l: self.gvec(0, l, c), acc=[4, 5, 6, 7])
                self.pairs = [0, 1, 2, 3]
                self.run_units(None, pr)
                self.fence()
                if int(os.environ.get('STOP', '99')) <= 1:
                    continue
                self.wslot = 0
                pr = self.prime(self.units_p2(s, l))
                self.stage_attn_a(s, l)
                self.fence()
                if int(os.environ.get('STOP', '99')) <= 2:
                    continue
                self.stage_attn_b(s, l)
                self.fence()
                if int(os.environ.get('STOP', '99')) <= 3:
                    continue
                self.run_units(None, pr)
                self.fence()
                if int(os.environ.get('STOP', '99')) <= 4:
                    continue
                for blk in range(2):
                    pr = self.prime(self.units_ffn(s, l, blk))
                    if FUSE_P2:
                        self.stage_norm(s, lambda c, l=l: self.gvec(1, l, c), t0=blk * 1024, nt=1024, ffn=True, acc=[4 + 2 * blk, 5 + 2 * blk])
                        self.pairs = [0, 1, 2] if blk == 0 else [0, 1, 3]
                    else:
                        self.pairs = [0, 1, 2, 3] if not FUSE_F2 else ([0, 1, 2, 3] if blk == 0 else [0, 1, 3])
                        self.stage_norm(s, lambda c, l=l: self.gvec(1, l, c), t0=blk * 1024, nt=1024, ffn=True)
                    self.run_units(None, pr)
                self.fence()
            if FUSE_F2:
                self.pairs = [0, 1]
                self.stage_out(s, acc=[4, 5, 6, 7])
            else:
                self.pairs = [0, 1, 2, 3]
                self.stage_out(s)
            self.pairs = [0, 1, 2, 3]
            self.fence()
        tr.finalize()
        return nc


def relayout(W, c0, ncols):
    K = W.shape[0]
    sub = W[:, c0:c0 + ncols].reshape(K // 128, 128, ncols // 128, 128)
    return np.ascontiguousarray(sub.transpose(2, 1, 0, 3)).reshape(ncols // 128, 128, K)


def host_consts(nl, norm_mix, norm_ffn, norm_final, gate_bias, diff_lambda, diff_subln):
    NF = 128 + 4 * 16 * nl + 16 + nl
    cf = np.zeros((128, NF), np.float32)
    cf[:, 0:128] = np.eye(128, dtype=np.float32)
    o = 128
    for arr in (norm_mix[:nl], norm_ffn[:nl]):
        cf[:, o:o + nl * 16] = arr.reshape(nl, 16, 128).transpose(2, 0, 1).reshape(128, nl * 16)
        o += nl * 16
    cf[:, o:o + nl * 32] = gate_bias[:nl].reshape(nl, 2, 16, 128).transpose(3, 0, 1, 2).reshape(128, nl * 32)
    o += nl * 32
    cf[:, o:o + 16] = norm_final.reshape(16, 128).T
    o += 16
    cf[:, o:o + nl] = diff_subln[:nl].T
    dl = np.ascontiguousarray(np.broadcast_to(diff_lambda[:nl].reshape(1, nl * 256), (128, nl * 256))).astype(np.float32)
    cb = np.zeros((128, 768), np.float32)
    cb[:, 0:128] = 1.0
    pa = np.zeros((128, 128), np.float32)
    for off in (0, 64):
        for i in range(8):
            pa[off + i + 8, off + i] = -1.0
            pa[off + i, off + i + 8] = 1.0
    pb = np.zeros((128, 128), np.float32)
    for i in range(16):
        pb[i + 16, i] = -1.0
        pb[i, i + 16] = 1.0
    cb[:, 128:256] = pa
    cb[:, 256:384] = pb
    kk = np.arange(128)[:, None]
    ql = np.arange(128)[None, :]
    cb[:, 384:512] = (kk >= ql)
    cb[:, 512:640] = (kk <= ql)
    cb[0:64, 640:768] = ((kk[0:64] + 64) >= ql)
    rope = np.zeros((4, 128, S), np.float32)
    rope[0] = 1.0
    rope[2] = 1.0
    pos = np.arange(S, dtype=np.float32)
    for ti, rot, offs in ((0, 16, (0, 64)), (2, 32, (0,))):
        inv = (np.float32(THETA) ** (-(np.arange(0, rot, 2, dtype=np.float32) / np.float32(rot)))).astype(np.float32)
        ang = (pos[:, None] * inv[None, :]).astype(np.float32)
        c, s_ = np.cos(ang).astype(np.float32).T, np.sin(ang).astype(np.float32).T
        half = rot // 2
        for off in offs:
            rope[ti, off:off + half] = c
            rope[ti, off + half:off + rot] = c
            rope[ti + 1, off:off + half] = s_
            rope[ti + 1, off + half:off + rot] = s_
    return cf, dl, cb, rope


def host_weights(nl, w_in, w_proj_a, w_proj_b, w_out, w_ffn_in, w_ffn_out):
    w_inF = np.empty((nl, NQK + NG, 128, 2048), np.float32)
    w_v = np.empty((nl, 5, 128, 8192), np.float32)
    w_pab = np.empty((nl, 16, 128, 1536), np.float32)
    w_o = np.empty((nl, 16, 128, 2048), np.float32)
    w_f1 = np.empty((nl, FC, 128, 4096), np.float32)
    w_f2 = np.empty((nl, 16, 128, FC * 128), np.float32)
    for l in range(nl):
        W = w_in[l]
        w_inF[l, 0:8] = relayout(W, 0, 1024)
        w_inF[l, 8:16] = relayout(W, 1024, 1024)
        w_inF[l, 16:28] = relayout(W, 3072, 1536)
        w_inF[l, 28:40] = relayout(W, 4608, 1536)
        w_inF[l, 40:72] = relayout(W, 7680, 4096)
        vcols = np.concatenate([W[:, 2048:3072], W[:, 6144:7680]], axis=1)
        w_v[l] = np.ascontiguousarray(vcols.reshape(16, 128, 5, 512).transpose(2, 1, 0, 3)).reshape(5, 128, 8192)
        w_pab[l, :, :, 0:1024] = relayout(w_proj_a[l], 0, 2048)
        w_pab[l, :, :, 1024:1536] = relayout(w_proj_b[l], 0, 2048)
        w_o[l] = relayout(w_out[l], 0, 2048)
        w_f1[l, :, :, 0:2048] = relayout(w_ffn_in[l], 0, FF)
        w_f1[l, :, :, 2048:4096] = relayout(w_ffn_in[l], FF, FF)
        w_f2[l] = relayout(w_ffn_out[l], 0, 2048)
    return dict(w_inF=w_inF, w_v=w_v, w_pab=w_pab, w_o=w_o, w_f1=w_f1, w_f2=w_f2)


_PROG_CACHE = {}


def run_cores(xs_per_core, nl, nseq, weights, consts):
    key = (nl, nseq)
    if key not in _PROG_CACHE:
        _PROG_CACHE[key] = Prog(nl, nseq).build()
    nc = _PROG_CACHE[key]
    cf, dl, cb, rope = consts
    in_maps = []
    for xc in xs_per_core:
        m = dict(x=np.ascontiguousarray(xc, dtype=np.float32), c_f32=cf, c_dlam=dl, c_bf=cb, c_rope=rope)
        m.update(weights)
        in_maps.append(m)
    res = run_bass_kernel_spmd(nc, in_maps, core_ids=list(range(len(in_maps))))
    return [r["y"] for r in res.results]


def kernel(x_prompt, x_sample, norm_mix, norm_ffn, w_in, gate_bias, diff_lambda, diff_subln,
           w_proj_a, w_proj_b, w_out, w_ffn_in, w_ffn_out, norm_final):
    f = lambda a: np.asarray(a, dtype=np.float32)
    x_prompt, x_sample = f(x_prompt), f(x_sample)
    weights = host_weights(NL, f(w_in), f(w_proj_a), f(w_proj_b), f(w_out), f(w_ffn_in), f(w_ffn_out))
    consts = host_consts(NL, f(norm_mix), f(norm_ffn), f(norm_final), f(gate_bias), f(diff_lambda), f(diff_subln))
    xs = []
    for c in range(4):
        xs.append(x_prompt[2 * c:2 * c + 2])
    for c in range(4):
        xs.append(np.stack([x_sample[c], x_sample[c]]))
    ys = run_cores(xs, NL, 2, weights, consts)
    y_prompt = np.concatenate([ys[c] for c in range(4)], axis=0)
    y_sample = np.stack([ys[4 + c][0] for c in range(4)], axis=0)
    return (y_prompt.astype(np.float32), y_sample.astype(np.float32))
```

```python
import math
import os
import numpy as np
import concourse.bass as bass
import concourse.mybir as mybir
from concourse.bass_utils import run_bass_kernel_spmd

F32 = mybir.dt.float32
BF16 = mybir.dt.bfloat16
AF = mybir.ActivationFunctionType
ALU = mybir.AluOpType

D = 2048
S = 2048
NL = 4
FF = 5632
KC = 16
FC = 44
EPS = 1e-6
FUSE_P2 = True
FUSE_F2 = False
NQK = 40
NG = 32
VW = 2560
B_DIL = (1, 4, 16)
THETA = 500000.0


def lam_init_for(l):
    return 0.8 - 0.6 * math.exp(-0.3 * l)


class Trk:
    def __init__(self, nc):
        self.nc = nc
        self.ops = []
        self.eng = {'pe': nc.tensor, 'act': nc.scalar, 'dve': nc.vector, 'pool': nc.gpsimd, 'sp': nc.sync}

    def add(self, eng, fn, reads=(), writes=(), dma=None):
        self.ops.append((eng, fn, tuple(reads), tuple(writes), dma))

    def fence(self, fn):
        self.ops.append(('sp', fn, ('__ALL__',), (), 'd_fence'))

    def finalize(self):
        nc = self.nc
        ops = self.ops
        n = len(ops)
        deps = [None] * n
        last_w = {}
        last_r = {}
        fence_i = None
        for i, (eng, fn, reads, writes, dma) in enumerate(ops):
            d = set()
            if reads == ('__ALL__',):
                d.update(last_w.values())
                for r in last_r.values():
                    d.update(r.values())
                if fence_i is not None:
                    d.add(fence_i)
                last_w = {}
                last_r = {}
                fence_i = i
                deps[i] = d
                continue
            if fence_i is not None:
                d.add(fence_i)
            for k in reads:
                w = last_w.get(k)
                if w is not None:
                    d.add(w)
            for k in writes:
                w = last_w.get(k)
                if w is not None:
                    d.add(w)
                r = last_r.get(k)
                if r:
                    d.update(r.values())
            for k in writes:
                last_w[k] = i
                last_r[k] = {}
            cls = ('dma', dma) if dma else eng
            for k in reads:
                lr = last_r.get(k)
                if lr is None:
                    lr = last_r[k] = {}
                lr[cls] = i
            d.discard(i)
            if eng == 'pe' and not dma:
                d = {j for j in d if not (ops[j][0] == 'pe' and not ops[j][4])}
            deps[i] = d
        needed = [False] * n
        for i in range(n):
            for j in deps[i]:
                needed[j] = True
        sems = {}

        def sem(name):
            if name not in sems:
                sems[name] = nc.alloc_semaphore(name=name)
            return sems[name]

        val = [0] * n
        semname = [None] * n
        cnt = {}
        for i, (eng, fn, reads, writes, dma) in enumerate(ops):
            if dma:
                cnt[dma] = cnt.get(dma, 0) + 16
                val[i] = cnt[dma]
                semname[i] = dma
            elif needed[i]:
                key = 'E_' + eng
                cnt[key] = cnt.get(key, 0) + 1
                val[i] = cnt[key]
                semname[i] = key
        waited = {e: {} for e in self.eng}
        issued = {}
        for i, (eng, fn, reads, writes, dma) in enumerate(ops):
            e = self.eng[eng]
            need = {}
            for j in deps[i]:
                sn = semname[j]
                if val[j] > need.get(sn, 0):
                    need[sn] = val[j]
            for sn, v in need.items():
                if waited[eng].get(sn, 0) >= v:
                    continue
                if not sn.startswith('E_'):
                    v = issued.get(sn, 0)
                e.wait_ge(sem(sn), v)
                waited[eng][sn] = v
            inst = fn()
            if dma:
                inst.then_inc(sem(dma), 16)
                issued[dma] = issued.get(dma, 0) + 16
            elif needed[i]:
                inst.then_inc(sem(semname[i]), 1)
        sp = self.eng['sp']
        for sn, v in cnt.items():
            if waited['sp'].get(sn, 0) < v:
                sp.wait_ge(sem(sn), v)
        self.nsems = len(sems)
        self.nops = n


class Prog:
    def __init__(self, nl=NL, nseq=2, dbg=False):
        self.nl = nl
        self.nseq = nseq
        nc = self.nc = bass.Bass("TRN2", target_bir_lowering=False)
        self.tr = Trk(nc)
        dt = nc.dram_tensor
        self.x_in = dt("x", [nseq, S, D], F32, kind="ExternalInput").ap()
        self.y_out = dt("y", [nseq, S, D], F32, kind="ExternalOutput").ap()
        self.w_inF = dt("w_inF", [nl, NQK + NG, 128, 2048], F32, kind="ExternalInput").ap()
        self.w_v = dt("w_v", [nl, 5, 128, 8192], F32, kind="ExternalInput").ap()
        self.w_pab = dt("w_pab", [nl, 16, 128, 1536], F32, kind="ExternalInput").ap()
        self.w_o = dt("w_o", [nl, 16, 128, 2048], F32, kind="ExternalInput").ap()
        self.w_f1 = dt("w_f1", [nl, FC, 128, 4096], F32, kind="ExternalInput").ap()
        self.w_f2 = dt("w_f2", [nl, 16, 128, FC * 128], F32, kind="ExternalInput").ap()
        self.NF = 128 + 4 * 16 * nl + 16 + nl
        self.c_f32 = dt("c_f32", [128, self.NF], F32, kind="ExternalInput").ap()
        self.c_dlam = dt("c_dlam", [128, nl * 256], F32, kind="ExternalInput").ap()
        self.c_bf = dt("c_bf", [128, 768], F32, kind="ExternalInput").ap()
        self.c_rope = dt("c_rope", [4, 128, S], F32, kind="ExternalInput").ap()
        self.xT = dt("xT", [nseq, D, S], F32, kind="Internal").ap()
        self.qkT = dt("qkT", [NQK * 128, S], BF16, kind="Internal").ap()
        self.gT = dt("gT", [NG * 128, S], BF16, kind="Internal").ap()
        self.vS = dt("vS", [S, VW], BF16, kind="Internal").ap()
        self._off = (nc.sbuf_base + 63) // 64 * 64
        self._top = nc.sbuf_top
        A = self.alloc
        self.XN = A("XN", [128, 16, S], BF16)
        self.W = A("W", [128, 2, 8192], BF16)
        self.XC = A("XC", [128, 2, 2048], F32)
        self.CF = A("CF", [128, self.NF], F32)
        self.CB = A("CB", [128, 768], BF16)
        self.LAM = A("LAM", [128, 2 * nl], F32)
        self.EPSB = A("EPSB", [128, 1], F32)
        self.FD = A("FD", [128, 16], F32)
        arena0 = self._off
        self.ROPE = A("ROPE", [128, 4, S], F32)
        self.RSTD = A("RSTD", [128, S], F32)
        self.SQ = A("SQ", [128, 2, S], BF16)
        self.ZB = A("ZB", [128, 2, 1024], BF16)
        self.T1 = A("T1", [128, 2, 1024], F32)
        self.T2 = A("T2", [128, 2, 1024], F32)
        self.OST = A("OST", [128, 2, 1024], BF16)
        self.VST = A("VST", [128, 2, 512], BF16)
        end1 = self._off
        self._off = arena0
        self.OA = A("OA", [128, 8, S], BF16)
        self.OB = A("OB", [128, 4, S], BF16)
        qt0 = self._off
        self.QT = A("QT", [128, 2, S], BF16)
        self.KT = A("KT", [128, 2, S], BF16)
        vt0 = self._off
        self.VT = A("VT", [128, 2, 16, 128], BF16)
        sub0 = self._off
        self.PT = A("PT", [128, 4, 512], BF16)
        self.EP = A("EP", [128, 6, 512], F32)
        self.SQA = A("SQA", [128, 512], BF16)
        self.PAB = A("PAB", [128, 2, 512], BF16)
        end2 = self._off
        self._off = vt0
        self.VO = A("VO", [128, 32, 128], BF16)
        self._off = sub0
        self.PB = A("PB", [128, 4, 128], BF16)
        self.OACC = A("OACC", [128, S], F32)
        self.ZACC = A("ZACC", [128, S], F32)
        end3 = self._off
        self._off = qt0
        self.GA = A("GA", [128, 2, 1024], BF16)
        self.GB = A("GB", [128, 2, 1024], BF16)
        self.M1 = A("M1", [128, 2, 1024], F32)
        self.M2 = A("M2", [128, 2, 1024], F32)
        self.SQR = A("SQR", [128, 2, 1024], BF16)
        end4 = self._off
        self._off = arena0
        self.ACTB = A("ACTB", [128, FC, 1024], BF16)
        self.SG = A("SG", [128, 1024], F32)
        end5 = self._off
        assert max(end1, end2, end3, end4, end5) <= self._top, (end1, end2, end3, end4, end5, self._top)
        self.PS = nc.alloc_psum_tensor("PS", [128, 8, 512], F32)
        self.wslot = 0
        self.pairs = [0, 1, 2, 3]
        self._quad = self._pi = self._si = 0
        self._pend_sq = None
        self._ob = 0
        self.phase = 0

    def alloc(self, name, shape, dtype):
        nbytes = int(np.prod(shape[1:])) * (4 if dtype == F32 else 2)
        t = self.nc.alloc_sbuf_tensor_at(name, shape, dtype, offset=self._off)
        self._off = (self._off + nbytes + 63) // 64 * 64
        return t

    def fence(self):
        nc = self.nc
        self.tr.fence(lambda: nc.sync.dma_start(out=self.FD[0:1, 0:16], in_=self.c_f32[0:1, 0:16]))

    def ak(self, name):
        return name

    def banks(self, n):
        P = self.pairs
        if True:
            for _ in range(32):
                b = self._ob
                if b + n > 8:
                    b = 0
                self._ob = (b + n) % 8
                if all((x // 2) in P for x in range(b, b + n)):
                    return list(range(b, b + n))
            raise RuntimeError("no free PSUM banks")
        if n == 4:
            quads = [q for q in (0, 1) if (2 * q in P and 2 * q + 1 in P)]
            q = quads[self._quad % len(quads)]
            self._quad += 1
            return [4 * q, 4 * q + 1, 4 * q + 2, 4 * q + 3]
        if n == 2:
            p = P[self._pi % len(P)]
            self._pi += 1
            return [2 * p, 2 * p + 1]
        idx = self._si % (2 * len(P))
        self._si += 1
        return [2 * P[idx // 2] + idx % 2]

    def ps(self, b0, nb=1):
        if nb == 1:
            return self.PS[:, b0, :]
        return self.PS[:, b0:b0 + nb, :]

    def mm(self, bank, out, lhsT, rhs, start, stop, reads, **kw):
        nc = self.nc
        self.tr.add('pe', lambda: nc.tensor.matmul(out, lhsT=lhsT, rhs=rhs, start=start, stop=stop, **kw),
                    reads=reads, writes=["PS%d" % bank])

    def dma(self, q, out, in_, sem, reads, writes):
        nc = self.nc
        e = {'sp': nc.sync, 'pool': nc.gpsimd}[q]
        self.tr.add(q, lambda: e.dma_start(out=out, in_=in_), reads=reads, writes=writes, dma=sem)

    def act(self, out, in_, func, reads, writes, bias=None, scale=None):
        nc = self.nc
        kw = {}
        if bias is not None:
            kw['bias'] = bias
        if scale is not None:
            kw['scale'] = scale
        self.tr.add('act', lambda: nc.scalar.activation(out=out, in_=in_, func=func, **kw), reads=reads, writes=writes)

    def tt(self, out, in0, in1, op, reads, writes):
        nc = self.nc
        self.tr.add('dve', lambda: nc.vector.tensor_tensor(out=out, in0=in0, in1=in1, op=op), reads=reads, writes=writes)

    def stt(self, out, in0, scalar, in1, op0, op1, reads, writes):
        nc = self.nc
        self.tr.add('dve', lambda: nc.vector.scalar_tensor_tensor(out=out, in0=in0, scalar=scalar, in1=in1, op0=op0, op1=op1),
                    reads=reads, writes=writes)

    def recip(self, out, in_, reads, writes):
        nc = self.nc
        self.tr.add('dve', lambda: nc.vector.reciprocal(out=out, in_=in_), reads=reads, writes=writes)

    def rfast(self, out, in_, reads, writes):
        self.recip(out, in_, reads, writes)

    def vcopy(self, out, in_, reads, writes):
        nc = self.nc
        self.tr.add('dve', lambda: nc.vector.tensor_copy(out=out, in_=in_), reads=reads, writes=writes)

    def ident(self):
        return self.CF[:, 0:128]

    def gvec(self, which, l, c):
        o = 128 + (which * self.nl + l) * 16 + c
        return self.CF[:, o:o + 1]

    def gbias(self, l, ab, c):
        o = 128 + 2 * 16 * self.nl + ((l * 2 + ab) * 16) + c
        return self.CF[:, o:o + 1]

    def gfinal(self, c):
        o = 128 + 4 * 16 * self.nl + c
        return self.CF[:, o:o + 1]

    def gsub_raw(self, l):
        o = 128 + 4 * 16 * self.nl + 16 + l
        return self.CF[:, o:o + 1]

    def setup(self):
        nc, tr, nl = self.nc, self.tr, self.nl
        self.dma('sp', self.CF[:], self.c_f32, 'd_cf', [], ['CF'])
        self.dma('pool', self.CB[:], self.c_bf, 'd_cb', [], ['CB'])
        DL = self.T1[:].rearrange("p a b -> p (a b)")[:, 0:nl * 256]
        PR = self.T2[:].rearrange("p a b -> p (a b)")[:, 0:nl * 128]
        SM = self.T2[:].rearrange("p a b -> p (a b)")[:, 1024:1024 + 2 * nl]
        EX = self.T2[:].rearrange("p a b -> p (a b)")[:, 1100:1100 + 2 * nl]
        self.dma('sp', DL, self.c_dlam, 'd_dl', [], ['DL'])
        dl4 = DL.rearrange("p (l f d) -> p l f d", f=4, d=64)
        pr3 = PR.rearrange("p (l t d) -> p l t d", t=2, d=64)
        for t in range(2):
            self.tt(pr3[:, :, t, :], dl4[:, :, 2 * t, :], dl4[:, :, 2 * t + 1, :], ALU.mult, ['DL'], ['PR%d' % t])
        tr.add('dve', lambda: nc.vector.reduce_sum(out=SM, in_=PR.rearrange("p (a d) -> p a d", d=64), axis=mybir.AxisListType.X),
               reads=['PR0', 'PR1'], writes=['SM'])
        self.act(EX, SM, AF.Exp, ['SM'], ['EX'])
        ex2 = EX.rearrange("p (l t) -> p l t", t=2)
        for l in range(nl):
            li = lam_init_for(l)
            o, i0, i1 = self.LAM[:, l:l + 1], ex2[:, l, 1:2], ex2[:, l, 0:1]
            tr.add('dve', (lambda o=o, i0=i0, i1=i1, li=li: nc.vector.scalar_tensor_tensor(
                out=o, in0=i0, scalar=-li, in1=i1, op0=ALU.add, op1=ALU.subtract)), reads=['EX'], writes=['LAM'])
            o2, g = self.LAM[:, nl + l:nl + l + 1], self.gsub_raw(l)
            tr.add('dve', (lambda o2=o2, g=g, li=li: nc.vector.tensor_scalar(
                out=o2, in0=g, scalar1=1.0 - li, scalar2=None, op0=ALU.mult)), reads=['CF'], writes=['LAM'])

    def stage_in(self, s):
        nc, tr = self.nc, self.tr
        for t in range(17):
            if t < 16:
                sl = t % 2
                self.dma('sp', self.XC[:, sl, :], self.x_in[s, t * 128:(t + 1) * 128, :], 'd_xc%d' % sl, [], ['XC%d' % sl])
            if t > 0:
                tt_ = t - 1
                sl = tt_ % 2
                bs = self.banks(4)
                for c in range(16):
                    b = bs[c // 4]
                    out = self.PS[:, b, (c % 4) * 128:(c % 4 + 1) * 128]
                    in_ = self.XC[:, sl, c * 128:(c + 1) * 128]
                    idn = self.ident()
                    tr.add('pe', (lambda out=out, in_=in_, idn=idn: nc.tensor.transpose(out, in_, idn)),
                           reads=['XC%d' % sl, 'CF'], writes=['PS%d' % b])
                st = tt_ % 2
                stg = self.T1[:, st, :] if False else None
                dst = self.ROPE[:, st, :]
                for q in range(4):
                    o = dst[:, q * 512:(q + 1) * 512]
                    i_ = self.PS[:, bs[q], :]
                    if q % 2 == 0:
                        self.act(o, i_, AF.Copy, ['PS%d' % bs[q]], ['ST0_%d' % st])
                    else:
                        self.vcopy(o, i_, ['PS%d' % bs[q]], ['ST0_%d' % st])
                dr = self.xT[s].rearrange("(c p) t -> p c t", p=128)[:, :, tt_ * 128:(tt_ + 1) * 128]
                self.dma('sp', dr, dst.rearrange("p (c t) -> p c t", t=128), 'd_st0_%d' % st, ['ST0_%d' % st], ['xT%d' % c for c in range(16)])

    def stage_norm(self, s, gfn, t0=0, nt=S, ffn=False, acc=None):
        nc, tr = self.nc, self.tr
        nb = nt // 512
        bs = acc if acc is not None else self.banks(nb)
        xsrc = self.xT[s].rearrange("(c p) t -> c p t", p=128)
        if ffn:
            rstd, rk = self.SG[:, 0:nt], 'SG'
            sq = [self.XC[:, i, 1024:2048].bitcast(BF16)[:, 0:nt] for i in range(2)]
            sqk = ['XR0', 'XR1']
        else:
            rstd, rk = self.RSTD[:, 0:nt], 'RSTD'
            sq = [self.SQ[:, i, 0:nt] for i in range(2)]
            sqk = ['SQ0', 'SQ1']
        for c in range(17 if acc is None else 0):
            if c < 16:
                sl = c % 2
                self.dma('sp', self.XC[:, sl, 0:nt], xsrc[c, :, t0:t0 + nt], 'd_xc%d' % sl, ['xT%d' % c], ['XC%d' % sl])
            if c > 0:
                cc = c - 1
                sl = cc % 2
                self.act(sq[sl], self.XC[:, sl, 0:nt], AF.Square, ['XC%d' % sl], [sqk[sl]])
                for b in range(nb):
                    self.mm(bs[b], self.PS[:, bs[b], :], self.CB[:, 0:128], sq[sl][:, b * 512:(b + 1) * 512],
                            cc == 0, cc == 15, [sqk[sl], 'CB'])
        psk = ['PS%d' % b for b in bs]
        rps = self.PS[:, bs[0]:bs[0] + nb, :]
        self.act(rstd.rearrange("p (b f) -> p b f", f=512), rps, AF.Sqrt, psk, [rk], bias=self.epsb(), scale=1.0 / D)
        self.rfast(rps, rstd.rearrange("p (b f) -> p b f", f=512), [rk], psk)
        for c in range(17):
            if c < 16:
                sl = c % 2
                self.dma('sp', self.XC[:, sl, 0:nt], xsrc[c, :, t0:t0 + nt], 'd_xc%d' % sl, ['xT%d' % c], ['XC%d' % sl])
            if c > 0:
                cc = c - 1
                sl = cc % 2
                self.stt(self.XN[:, cc, 0:nt].rearrange("p (b f) -> p b f", f=512), self.XC[:, sl, 0:nt].rearrange("p (b f) -> p b f", f=512),
                         gfn(cc), rps, ALU.mult, ALU.mult, ['XC%d' % sl, 'CF'] + psk, ['XN%d' % cc])

    def epsb(self):
        return self.EPSB[:, 0:1]

    def wgroups(self, wd, nch, E):
        G = max(1, 8192 // E)
        R = E
        while R > 2048:
            R //= 2
        if E == FC * 128:
            R = 1408
        assert E % R == 0
        for g0 in range(0, nch, G):
            n = min(G, nch - g0)
            slot = self.wslot
            self.wslot ^= 1

            def load(slot=slot, g0=g0, n=n):
                out = self.W[:, slot, 0:n * E].rearrange("p (n a r) -> p n a r", n=n, r=R)
                in_ = wd[g0:g0 + n].rearrange("n p (a r) -> p n a r", r=R)
                self.dma('pool', out, in_, 'd_w%d' % slot, [], ['W%d' % slot])
            yield load, [(slot, i * E) for i in range(n)]

    def wtile(self, slot, off, kc, width=128, kstride=128):
        o = off + kc * kstride
        return self.W[:, slot, o:o + width]

    def units_p1(self, s, l):
        nc, tr = self.nc, self.tr
        def loadrope():
            self.dma('sp', self.ROPE[:].rearrange("p a t -> p a t"), self.c_rope.rearrange("a p t -> p a t"), 'd_rope', [], ['ROPE'])
        first = [True]
        pend = [None]
        for load, tiles in self.wgroups(self.w_inF[l], NQK + NG, 2048):
            for gi, (slot, off) in enumerate(tiles):
                j = self._p1_j
                self._p1_j += 1

                def comp(j=j, slot=slot, off=off):
                    if j == 0:
                        loadrope()
                    if j < NQK:
                        self.p1_qk(s, l, j, slot, off, pend)
                    else:
                        if pend[0] is not None:
                            pend[0]()
                            pend[0] = None
                        self.p1_gate(s, l, j - NQK, slot, off)
                yield (load if gi == 0 else None), comp
        for load, tiles in self.wgroups(self.w_v[l], 5, 8192):
            for gi, (slot, off) in enumerate(tiles):
                vb = self._p1_v
                self._p1_v += 1

                def compv(vb=vb, slot=slot, off=off):
                    for t in range(16):
                        b = self.banks(1)[0]
                        for kc in range(16):
                            self.mm(b, self.PS[:, b, :], self.XN[:, kc, t * 128:(t + 1) * 128], self.W[:, slot, off + kc * 512:off + (kc + 1) * 512],
                                    kc == 0, kc == 15, ['XN%d' % kc, 'W%d' % slot])
                        st = self._vst
                        self._vst ^= 1
                        if t % 2 == 0:
                            self.act(self.VST[:, st, :], self.PS[:, b, :], AF.Copy, ['PS%d' % b], ['VST%d' % st])
                        else:
                            self.vcopy(self.VST[:, st, :], self.PS[:, b, :], ['PS%d' % b], ['VST%d' % st])
                        self.dma('sp', self.vS[t * 128:(t + 1) * 128, vb * 512:(vb + 1) * 512], self.VST[:, st, :], 'd_vst%d' % st,
                                 ['VST%d' % st], ['vS'])
                yield (load if gi == 0 else None), compv

    def p1_qk(self, s, l, j, slot, off, pend):
        nc, tr = self.nc, self.tr
        isB = j >= 16
        ct = 2 if isB else 0
        perm = self.CB[:, 256:384] if isB else self.CB[:, 128:256]
        grp = None
        if isB:
            jj = (j - 16) % 12
            grp = jj // 4
        for hf in range(2):
            zb_ = self.banks(2)
            for kc in range(16):
                for tb in range(2):
                    self.mm(zb_[tb], self.PS[:, zb_[tb], :], self.wtile(slot, off, kc), self.XN[:, kc, hf * 1024 + tb * 512:hf * 1024 + (tb + 1) * 512],
                            kc == 0, kc == 15, ['XN%d' % kc, 'W%d' % slot])
            if pend[0] is not None:
                pend[0]()
                pend[0] = None
            sl = self._zsl
            self._zsl ^= 1
            zps = self.PS[:, zb_[0]:zb_[0] + 2, :]
            zkeys = ['PS%d' % b for b in zb_]
            tsl = slice(hf * 1024, (hf + 1) * 1024)
            v3 = lambda ap: ap.rearrange("p (b f) -> p b f", f=512)
            self.act(v3(self.ZB[:, sl, :]), zps, AF.Copy, zkeys, ['ZB%d' % sl])
            self.tt(v3(self.T1[:, sl, :]), zps, v3(self.ROPE[:, ct, tsl]), ALU.mult, zkeys + ['ROPE'], ['T1_%d' % sl])

            def rest(j=j, hf=hf, sl=sl, perm=perm, ct=ct, grp=grp, tsl=tsl, v3=v3):
                pb_ = self.banks(2)
                for tb in range(2):
                    self.mm(pb_[tb], self.PS[:, pb_[tb], :], perm, self.ZB[:, sl, tb * 512:(tb + 1) * 512], True, True, ['ZB%d' % sl, 'CB'])
                pps = self.PS[:, pb_[0]:pb_[0] + 2, :]
                self.tt(v3(self.T2[:, sl, :]), pps, v3(self.ROPE[:, ct + 1, tsl]), ALU.mult, ['PS%d' % b for b in pb_] + ['ROPE'], ['T2_%d' % sl])
                o = self.OST[:, sl, :]
                if grp is None or B_DIL[grp] == 1:
                    oo = o
                    dst = self.qkT[j * 128:(j + 1) * 128, hf * 1024:(hf + 1) * 1024]
                    self.tt(oo, self.T1[:, sl, :], self.T2[:, sl, :], ALU.add, ['T1_%d' % sl, 'T2_%d' % sl], ['OST%d' % sl])
                    self.dma('sp', dst, o, 'd_ost%d' % sl, ['OST%d' % sl], ['qkT'])
                else:
                    d = B_DIL[grp]
                    oo = o.rearrange("p (q m) -> p m q", q=d)
                    i1 = self.T1[:, sl, :].rearrange("p (m q) -> p m q", q=d)
                    i2 = self.T2[:, sl, :].rearrange("p (m q) -> p m q", q=d)
                    self.tt(oo, i1, i2, ALU.add, ['T1_%d' % sl, 'T2_%d' % sl], ['OST%d' % sl])
                    nsub = S // d
                    hl = 1024 // d
                    dst = self.qkT[j * 128:(j + 1) * 128, :].rearrange("r (q m) -> r q m", q=d)[:, :, hf * hl:(hf + 1) * hl]
                    self.dma('sp', dst, o.rearrange("p (q m) -> p q m", q=d), 'd_ost%d' % sl, ['OST%d' % sl], ['qkT'])
            pend[0] = rest

    def p1_gate(self, s, l, g, slot, off):
        ab, c = g // 16, g % 16
        for hf in range(2):
            zb_ = self.banks(2)
            for kc in range(16):
                for tb in range(2):
                    self.mm(zb_[tb], self.PS[:, zb_[tb], :], self.wtile(slot, off, kc), self.XN[:, kc, hf * 1024 + tb * 512:hf * 1024 + (tb + 1) * 512],
                            kc == 0, kc == 15, ['XN%d' % kc, 'W%d' % slot])
            sl = self._zsl
            self._zsl ^= 1
            self.act(self.OST[:, sl, :].rearrange("p (b f) -> p b f", f=512), self.PS[:, zb_[0]:zb_[0] + 2, :], AF.Sigmoid,
                     ['PS%d' % b for b in zb_] + ['CF'], ['OST%d' % sl], bias=self.gbias(l, ab, c))
            self.dma('sp', self.gT[g * 128:(g + 1) * 128, hf * 1024:(hf + 1) * 1024], self.OST[:, sl, :], 'd_ost%d' % sl, ['OST%d' % sl], ['gT'])

    def stage_attn_a(self, s, l):
        nc, tr, nl = self.nc, self.tr, self.nl
        scale = 64 ** -0.5
        neglam = self.LAM[:, l:l + 1]
        gsub = self.LAM[:, nl + l:nl + l + 1]
        ones = self.CB[:, 0:128]
        E = self.EP
        OC = [E[:, 0, :], E[:, 1, :]]
        RC = [E[:, 2, :], E[:, 3, :]]
        SS = self.OB[:, 0:2, :].rearrange("p a t -> p (a t)").bitcast(F32)

        def loads(h):
            sl = h % 2
            self.dma('sp', self.QT[:, sl, :], self.qkT[h * 128:(h + 1) * 128, :], 'd_qt%d' % sl, ['qkT'], ['QT%d' % sl])
            self.dma('sp', self.KT[:, sl, :], self.qkT[(8 + h) * 128:(9 + h) * 128, :], 'd_kt%d' % sl, ['qkT'], ['KT%d' % sl])
            self.dma('sp', self.VT[:, sl, :, :], self.vS[:, h * 128:(h + 1) * 128].rearrange("(t p) e -> p t e", p=128), 'd_vt%d' % sl,
                     ['vS'], ['VT%d' % sl])
        loads(0)
        pend = [None]
        for h in range(8):
            if h + 1 < 8:
                loads(h + 1)
            sl = h % 2
            qk = ['QT%d' % sl, 'KT%d' % sl]
            for qb in range(4):
                qs = slice(qb * 512, (qb + 1) * 512)

                def s_step(kt):
                    sb = 2 * (kt % 2)
                    for c in range(2):
                        self.mm(sb + c, self.PS[:, sb + c, :], self.KT[64 * c:64 * c + 64, sl, kt * 128:(kt + 1) * 128],
                                self.QT[64 * c:64 * c + 64, sl, qs], True, True, qk)
                    self.act(self.PT[:, sb:sb + 2, :], self.PS[:, sb:sb + 2, :], AF.Exp, ['PS%d' % sb, 'PS%d' % (sb + 1)],
                             ['PT%d' % sb, 'PT%d' % (sb + 1)], scale=scale)

                def av_step(kt):
                    for c in range(2):
                        p = (kt % 2) * 2 + c
                        self.mm(4 + 2 * c, self.PS[:, 4 + 2 * c, :], self.VT[:, sl, kt, :], self.PT[:, p, :], kt == 0, kt == 15, ['VT%d' % sl, 'PT%d' % p])
                        self.mm(5 + 2 * c, self.PS[:, 5 + 2 * c, :], ones, self.PT[:, p, :], kt == 0, kt == 15, ['CB', 'PT%d' % p])
                s_step(0)
                for kt in range(16):
                    if kt + 1 < 16:
                        s_step(kt + 1)
                    av_step(kt)
                    if kt == 4 and pend[0] is not None:
                        pend[0]()
                        pend[0] = None
                for c in range(2):
                    self.vcopy(RC[c], self.PS[:, 5 + 2 * c, :], ['PS%d' % (5 + 2 * c)], ['RC%d' % c])
                    self.act(OC[c], self.PS[:, 4 + 2 * c, :], AF.Copy, ['PS%d' % (4 + 2 * c)], ['OC%d' % c])
                for c in range(2):
                    self.recip(RC[c], RC[c], ['RC%d' % c], ['RC%d' % c])
                    self.tt(OC[c], OC[c], RC[c], ALU.mult, ['OC%d' % c, 'RC%d' % c], ['OC%d' % c])
                self.stt(OC[0], OC[1], neglam, OC[0], ALU.mult, ALU.add, ['OC0', 'OC1', 'LAM'], ['OC0'])
                self.tt(self.SQA[:, :], OC[0], OC[0], ALU.mult, ['OC0'], ['SQA'])

                def part2(h=h, qs=qs, qb=qb):
                    self.mm(0, self.PS[:, 0, :], ones, self.SQA[:, :], True, True, ['SQA', 'CB'])
                    self.vcopy(SS[:, qs], self.PS[:, 0, :], ['PS0'], ['SS'])
                    oah = self.OA[:, h, qs]
                    tr.add('dve', (lambda oah=oah: nc.vector.tensor_scalar(out=oah, in0=OC[0], scalar1=gsub, scalar2=None, op0=ALU.mult)),
                           reads=['OC0', 'LAM'], writes=['OA%d' % h])
                    if qb == 3:
                        self.act(SS[:, :], SS[:, :], AF.Sqrt, ['SS'], ['SS'], bias=self.epsb(), scale=1.0 / 128)
                        self.recip(SS[:, :], SS[:, :], ['SS'], ['SS'])
                        self.tt(self.OA[:, h, :], self.OA[:, h, :], SS[:, :], ALU.mult, ['OA%d' % h, 'SS'], ['OA%d' % h])
                pend[0] = part2
        pend[0]()

    def stage_attn_b(self, s, l):
        nc, tr = self.nc, self.tr
        scale = 128 ** -0.5
        ones = self.CB[:, 0:128]
        mHi, mLo, mFi = self.CB[:, 384:512], self.CB[:, 512:640], self.CB[:, 640:768]
        mBoth = self.CB[:, 384:640]
        VOs = [self.VO, self.W[:, 1, 0:4096].rearrange("p (t e) -> p t e", e=128)]
        combos = [(h, g) for h in range(4) for g in range(3)]

        def loads(idx):
            h, g = combos[idx]
            d = B_DIL[g]
            nsub = S // d
            J = nsub // 128
            jq = 16 + g * 4 + h
            jk = 28 + g * 4 + h
            sl = idx % 2
            self.dma('sp', self.QT[:, sl, :], self.qkT[jq * 128:(jq + 1) * 128, :], 'd_qt%d' % sl, ['qkT'], ['QT%d' % sl])
            self.dma('sp', self.KT[:, sl, :], self.qkT[jk * 128:(jk + 1) * 128, :], 'd_kt%d' % sl, ['qkT'], ['KT%d' % sl])
            vcol = 1024 + (g * 4 + h) * 128
            vsrc = self.vS[:, vcol:vcol + 128]
            VO = VOs[sl]
            nt_ = J + 1
            vk, vsem = 'VO%d' % sl, 'd_vo%d' % sl
            for p in range(d):
                if J > 1:
                    src = vsrc[(64 * d + p):(64 * d + p) + ((J - 1) * 128 - 1) * d + 1:d, :].rearrange("(j k) e -> k j e", k=128)
                    self.dma('sp', VO[:, p * nt_ + 1:p * nt_ + J, :], src, vsem, ['vS'], [vk])
            srcf = vsrc[0:64 * d, :].rearrange("(k p) e -> k p e", p=d)
            self.dma('sp', VO[0:64, 0:(d - 1) * nt_ + 1:nt_, :], srcf, vsem, ['vS'], [vk])
            srcl = vsrc[(nsub - 64) * d:nsub * d, :].rearrange("(k p) e -> k p e", p=d)
            self.dma('sp', VO[0:64, J:J + (d - 1) * nt_ + 1:nt_, :], srcl, vsem, ['vS'], [vk])

        loads(0)
        for idx, (h, g) in enumerate(combos):
            if idx + 1 < len(combos):
                loads(idx + 1)
            d = B_DIL[g]
            nsub = S // d
            J = nsub // 128
            sl = idx % 2
            VO = VOs[sl]
            vk = 'VO%d' % sl
            nt_ = J + 1
            qk = ['QT%d' % sl, 'KT%d' % sl]
            blocks = [(p, u) for p in range(d) for u in range(J)]
            info = {}

            def st1(i):
                p, u = blocks[i]
                q0 = p * nsub + u * 128
                sb = self.banks(1)[0]
                tiles = []
                for ti, j in enumerate((u, u + 1)):
                    if j == 0:
                        k0, M, mask = p * nsub, 64, mFi[0:64, :]
                    elif j == J:
                        k0, M, mask = p * nsub + nsub - 64, 64, mLo[0:64, :]
                    else:
                        k0, M, mask = p * nsub + 128 * j - 64, 128, (mHi if ti == 0 else mLo)
                    tiles.append((j, k0, M, mask))
                for ti, (j, k0, M, mask) in enumerate(tiles):
                    self.mm(sb, self.PS[0:M, sb, ti * 128:(ti + 1) * 128], self.KT[:, sl, k0:k0 + M], self.QT[:, sl, q0:q0 + 128], True, True, qk)
                pp = self._pb
                self._pb = (self._pb + 2) % 4
                pks = ['PB%d' % pp, 'PB%d' % (pp + 1)]
                if tiles[0][2] == 128 and tiles[1][2] == 128:
                    pb2 = self.PB[:, pp:pp + 2, :]
                    self.act(pb2, self.PS[:, sb, 0:256].rearrange("p (a b) -> p a b", b=128), AF.Exp, ['PS%d' % sb], pks, scale=scale)
                    self.tt(pb2, pb2, mBoth.rearrange("p (a b) -> p a b", b=128), ALU.mult, pks + ['CB'], pks)
                else:
                    for ti, (j, k0, M, mask) in enumerate(tiles):
                        self.act(self.PB[0:M, pp + ti, :], self.PS[0:M, sb, ti * 128:(ti + 1) * 128], AF.Exp, ['PS%d' % sb], [pks[ti]], scale=scale)
                        self.tt(self.PB[0:M, pp + ti, :], self.PB[0:M, pp + ti, :], mask, ALU.mult, [pks[ti], 'CB'], [pks[ti]])
                info[i] = (tiles, pp)

            def st2(i):
                p, u = blocks[i]
                tiles, pp = info.pop(i)
                ob = self.banks(1)[0]
                for ti, (j, k0, M, mask) in enumerate(tiles):
                    self.mm(ob, self.PS[:, ob, 0:128], VO[0:M, p * nt_ + j, :], self.PB[0:M, pp + ti, :], ti == 0, ti == 1, [vk, 'PB%d' % (pp + ti)])
                for ti, (j, k0, M, mask) in enumerate(tiles):
                    self.mm(ob, self.PS[:, ob, 128:256], ones[0:M, :], self.PB[0:M, pp + ti, :], ti == 0, ti == 1, ['CB', 'PB%d' % (pp + ti)])
                st = (u * 128) * d + p
                oa = self.OACC[:, st:st + 127 * d + 1:d]
                za = self.ZACC[:, st:st + 127 * d + 1:d]
                if g == 0:
                    self.vcopy(oa, self.PS[:, ob, 0:128], ['PS%d' % ob], ['OACC'])
                    self.vcopy(za, self.PS[:, ob, 128:256], ['PS%d' % ob], ['ZACC'])
                else:
                    self.tt(oa, self.PS[:, ob, 0:128], oa, ALU.add, ['PS%d' % ob, 'OACC'], ['OACC'])
                    self.tt(za, self.PS[:, ob, 128:256], za, ALU.add, ['PS%d' % ob, 'ZACC'], ['ZACC'])

            st1(0)
            for i in range(len(blocks)):
                if i + 1 < len(blocks):
                    st1(i + 1)
                st2(i)
            if g == 2:
                self.rfast(self.ZACC[:, :], self.ZACC[:, :], ['ZACC'], ['ZACC'])
                self.tt(self.OB[:, h, :], self.OACC[:, :], self.ZACC[:, :], ALU.mult, ['OACC', 'ZACC'], ['OB%d' % h])

    def units_p2(self, s, l):
        for load, tiles in self.wgroups(self.w_pab[l], 16, 1536):
            for gi, (slot, off) in enumerate(tiles):
                c = self._p2_c
                self._p2_c += 1

                def comp(c=c, slot=slot, off=off):
                    for hf in range(2):
                        sl = self._gsl
                        self._gsl ^= 1
                        ts_ = slice(hf * 1024, (hf + 1) * 1024)
                        self.dma('sp', self.GA[:, sl, :], self.gT[c * 128:(c + 1) * 128, ts_], 'd_ga%d' % sl, ['gT'], ['GA%d' % sl])
                        self.dma('sp', self.GB[:, sl, :], self.gT[(16 + c) * 128:(17 + c) * 128, ts_], 'd_gb%d' % sl, ['gT'], ['GB%d' % sl])
                        ya = self.banks(2)
                        for kc in range(8):
                            for tb in range(2):
                                self.mm(ya[tb], self.PS[:, ya[tb], :], self.wtile(slot, off, kc), self.OA[:, kc, hf * 1024 + tb * 512:hf * 1024 + (tb + 1) * 512],
                                        kc == 0, kc == 7, ['OA%d' % kc, 'W%d' % slot])
                        yb = self.banks(2)
                        for kc in range(4):
                            for tb in range(2):
                                self.mm(yb[tb], self.PS[:, yb[tb], :], self.wtile(slot, off, 8 + kc), self.OB[:, kc, hf * 1024 + tb * 512:hf * 1024 + (tb + 1) * 512],
                                        kc == 0, kc == 3, ['OB%d' % kc, 'W%d' % slot])
                        v3 = lambda ap: ap.rearrange("p (b f) -> p b f", f=512)
                        self.tt(v3(self.M1[:, sl, :]), self.PS[:, ya[0]:ya[0] + 2, :], v3(self.GA[:, sl, :]), ALU.mult,
                                ['PS%d' % b for b in ya] + ['GA%d' % sl], ['M1_%d' % sl])
                        self.tt(v3(self.M2[:, sl, :]), self.PS[:, yb[0]:yb[0] + 2, :], v3(self.GB[:, sl, :]), ALU.mult,
                                ['PS%d' % b for b in yb] + ['GB%d' % sl], ['M2_%d' % sl])
                        self.tt(self.XN[:, c, ts_], self.M1[:, sl, :], self.M2[:, sl, :], ALU.add, ['M1_%d' % sl, 'M2_%d' % sl], ['XN%d' % c])
                yield (load if gi == 0 else None), comp
        if FUSE_P2:
            self.pairs = [0, 1]
        yield from self.units_resid(s, self.w_o[l], 2048, 16, self.XN, 'XN', 0, S, acc=([[4, 5], [6, 7]] if FUSE_P2 else None),
                                    sqb=[self.SQR[:, 0, :], self.SQR[:, 1, :]])

    def units_resid(self, s, wd, E, Kc, act, actkey, t0, nt, acc=None, sqb=None):
        xsrc = self.xT[s].rearrange("(c p) t -> c p t", p=128)
        nh = nt // 1024
        ones = self.CB[:, 0:128]
        for load, tiles in self.wgroups(wd, 16, E):
            for gi, (slot, off) in enumerate(tiles):
                c = self._r_c % 16
                self._r_c += 1

                def comp(c=c, slot=slot, off=off):
                    for hf in range(nh):
                        sl = self._xsl
                        self._xsl ^= 1
                        ts_ = slice(t0 + hf * 1024, t0 + (hf + 1) * 1024)
                        self.dma('sp', self.XC[:, sl, 0:1024], xsrc[c, :, ts_], 'd_xc%d' % sl, ['xT%d' % c], ['XC%d' % sl])
                        zb_ = self.banks(2)
                        for kc in range(Kc):
                            for tb in range(2):
                                self.mm(zb_[tb], self.PS[:, zb_[tb], :], self.wtile(slot, off, kc), act[:, kc, hf * 1024 + tb * 512:hf * 1024 + (tb + 1) * 512],
                                        kc == 0, kc == Kc - 1, ['%s%d' % (actkey, kc), 'W%d' % slot])
                        if self._pend_sq is not None:
                            self._pend_sq()
                            self._pend_sq = None
                        self.tt(self.XC[:, sl, 1024:2048].rearrange("p (b f) -> p b f", f=512), self.PS[:, zb_[0]:zb_[0] + 2, :],
                                self.XC[:, sl, 0:1024].rearrange("p (b f) -> p b f", f=512), ALU.add,
                                ['PS%d' % b for b in zb_] + ['XC%d' % sl], ['XR%d' % sl])
                        self.dma('sp', xsrc[c, :, ts_], self.XC[:, sl, 1024:2048], 'd_xr%d' % sl, ['XR%d' % sl], ['xT%d' % c])
                        if acc is not None:
                            self.act(sqb[sl], self.XC[:, sl, 1024:2048], AF.Square, ['XR%d' % sl], ['SQR%d' % sl])

                            def f(c=c, hf=hf, sl=sl):
                                for tb in range(2):
                                    b = acc[hf][tb]
                                    self.mm(b, self.PS[:, b, :], ones, sqb[sl][:, tb * 512:(tb + 1) * 512], c == 0, c == 15, ['SQR%d' % sl, 'CB'])
                            if c == 15 and hf == nh - 1:
                                f()
                            else:
                                self._pend_sq = f
                yield (load if gi == 0 else None), comp

    def units_ffn(self, s, l, blk):
        t0 = blk * 1024
        for load, tiles in self.wgroups(self.w_f1[l], FC, 4096):
            for gi, (slot, off) in enumerate(tiles):
                j = self._f_j % FC
                self._f_j += 1

                def comp(j=j, slot=slot, off=off):
                    gb_ = self.banks(2)
                    for kc in range(16):
                        for tb in range(2):
                            self.mm(gb_[tb], self.PS[:, gb_[tb], :], self.wtile(slot, off, kc), self.XN[:, kc, tb * 512:(tb + 1) * 512],
                                    kc == 0, kc == 15, ['XN%d' % kc, 'W%d' % slot])
                    ub_ = self.banks(2)
                    for kc in range(16):
                        for tb in range(2):
                            self.mm(ub_[tb], self.PS[:, ub_[tb], :], self.wtile(slot, off + 2048, kc), self.XN[:, kc, tb * 512:(tb + 1) * 512],
                                    kc == 0, kc == 15, ['XN%d' % kc, 'W%d' % slot])
                    sg = self.SG[:, :].rearrange("p (b f) -> p b f", f=512)
                    self.act(sg, self.PS[:, gb_[0]:gb_[0] + 2, :], AF.Silu, ['PS%d' % b for b in gb_], ['SG'])
                    self.tt(self.ACTB[:, j, :].rearrange("p (b f) -> p b f", f=512), self.PS[:, ub_[0]:ub_[0] + 2, :], sg, ALU.mult,
                            ['PS%d' % b for b in ub_] + ['SG'], ['ACTB%d' % j])
                yield (load if gi == 0 else None), comp
        if FUSE_F2:
            self.pairs = [0, 1]
        yield from self.units_resid(s, self.w_f2[l], FC * 128, FC, self.ACTB, 'ACTB', t0, 1024, acc=([[4 + 2 * blk, 5 + 2 * blk]] if FUSE_F2 else None),
                                    sqb=[self.XN[:, 14, 1024:2048], self.XN[:, 15, 1024:2048]])

    def stage_out(self, s, acc=None):
        nc, tr = self.nc, self.tr
        xsrc = self.xT[s].rearrange("(c p) t -> c p t", p=128)
        bs = acc if acc is not None else self.banks(4)
        for c in range(17 if acc is None else 0):
            if c < 16:
                sl = c % 2
                self.dma('sp', self.XC[:, sl, :], xsrc[c], 'd_xc%d' % sl, ['xT%d' % c], ['XC%d' % sl])
            if c > 0:
                cc = c - 1
                sl = cc % 2
                self.act(self.SQ[:, sl, :], self.XC[:, sl, :], AF.Square, ['XC%d' % sl], ['SQ%d' % sl])
                for b in range(4):
                    self.mm(bs[b], self.PS[:, bs[b], :], self.CB[:, 0:128], self.SQ[:, sl, b * 512:(b + 1) * 512], cc == 0, cc == 15, ['SQ%d' % sl, 'CB'])
        self.act(self.RSTD[:].rearrange("p (b f) -> p b f", f=512), self.PS[:, bs[0]:bs[0] + 4, :], AF.Sqrt,
                 ['PS%d' % b for b in bs], ['RSTD'], bias=self.epsb(), scale=1.0 / D)
        self.recip(self.RSTD[:], self.RSTD[:], ['RSTD'], ['RSTD'])
        self.pairs = [0, 1, 2, 3]
        YN = self.ROPE
        for c in range(17):
            if c < 16:
                sl = c % 2
                self.dma('sp', self.XC[:, sl, :], xsrc[c], 'd_xc%d' % sl, ['xT%d' % c], ['XC%d' % sl])
            if c > 0:
                cc = c - 1
                sl = cc % 2
                self.stt(YN[:, sl, :], self.XC[:, sl, :], self.gfinal(cc), self.RSTD[:], ALU.mult, ALU.mult, ['XC%d' % sl, 'RSTD', 'CF'], ['YN%d' % sl])
                tb_ = self.banks(4)
                for t in range(16):
                    b = tb_[t // 4]
                    out = self.PS[:, b, (t % 4) * 128:(t % 4 + 1) * 128]
                    in_ = YN[:, sl, t * 128:(t + 1) * 128]
                    idn = self.ident()
                    tr.add('pe', (lambda out=out, in_=in_, idn=idn: nc.tensor.transpose(out, in_, idn)), reads=['YN%d' % sl, 'CF'], writes=['PS%d' % b])
                dst = YN[:, 2 + sl, :]
                for q in range(4):
                    o = dst[:, q * 512:(q + 1) * 512]
                    if q % 2 == 0:
                        self.act(o, self.PS[:, tb_[q], :], AF.Copy, ['PS%d' % tb_[q]], ['YT%d' % sl])
                    else:
                        self.vcopy(o, self.PS[:, tb_[q], :], ['PS%d' % tb_[q]], ['YT%d' % sl])
                dr = self.y_out[s].rearrange("(t p) f -> p t f", p=128)[:, :, cc * 128:(cc + 1) * 128]
                self.dma('sp', dr, dst.rearrange("p (t f) -> p t f", f=128), 'd_yt%d' % sl, ['YT%d' % sl], ['y'])

    def prime(self, gen):
        load, comp = next(gen)
        if load is not None:
            load()
        return comp, gen

    def run_units(self, gen, primed=None):
        prev = None
        if primed is not None:
            prev, gen = primed
        for load, comp in (gen if gen is not None else ()):
            if load is not None:
                load()
            if prev is not None:
                prev()
            prev = comp
        if prev is not None:
            prev()

    def build(self):
        nc, tr = self.nc, self.tr
        self._zsl = self._vst = self._pb = self._gsl = self._xsl = self._ssl = 0
        self._p1_j = self._p1_v = self._p2_c = self._r_c = self._f_j = 0
        tr.add('dve', lambda: nc.vector.memset(self.EPSB[:], EPS), writes=['EPSB'])
        self.setup()
        self.fence()
        for s in range(self.nseq):
            self.stage_in(s)
        self.fence()
        for s in range(self.nseq):
            for l in range(self.nl):
                self._p1_j = self._p1_v = self._p2_c = 0
                pr = self.prime(self.units_p1(s, l))
                if l == 0 or not FUSE_F2:
                    self.pairs = [0, 1, 2, 3]
                    self.stage_norm(s, lambda c, l=l: self.gvec(0, l, c))
                else:
                    self.stage_norm(s, lambda c, l=l: self.gvec(0, l, c), acc=[4, 5, 6, 7])
                self.pairs = [0, 1, 2, 3]
                self.run_units(None, pr)
                self.fence()
                if 99 <= 1:
                    continue
                self.wslot = 0
                pr = self.prime(self.units_p2(s, l))
                self.stage_attn_a(s, l)
                self.fence()
                if 99 <= 2:
                    continue
                self.stage_attn_b(s, l)
                self.fence()
                if 99 <= 3:
                    continue
                self.run_units(None, pr)
                self.fence()
                if 99 <= 4:
                    continue
                for blk in range(2):
                    pr = self.prime(self.units_ffn(s, l, blk))
                    if FUSE_P2:
                        self.stage_norm(s, lambda c, l=l: self.gvec(1, l, c), t0=blk * 1024, nt=1024, ffn=True, acc=[4 + 2 * blk, 5 + 2 * blk])
                        self.pairs = [0, 1, 2] if blk == 0 else [0, 1, 3]
                    else:
                        self.pairs = [0, 1, 2, 3] if not FUSE_F2 else ([0, 1, 2, 3] if blk == 0 else [0, 1, 3])
                        self.stage_norm(s, lambda c, l=l: self.gvec(1, l, c), t0=blk * 1024, nt=1024, ffn=True)
                    self.run_units(None, pr)
                self.fence()
            if FUSE_F2:
                self.pairs = [0, 1]
                self.stage_out(s, acc=[4, 5, 6, 7])
            else:
                self.pairs = [0, 1, 2, 3]
                self.stage_out(s)
            self.pairs = [0, 1, 2, 3]
            self.fence()
        tr.finalize()
        return nc


def relayout(W, c0, ncols):
    K = W.shape[0]
    sub = W[:, c0:c0 + ncols].reshape(K // 128, 128, ncols // 128, 128)
    return np.ascontiguousarray(sub.transpose(2, 1, 0, 3)).reshape(ncols // 128, 128, K)


def host_consts(nl, norm_mix, norm_ffn, norm_final, gate_bias, diff_lambda, diff_subln):
    NF = 128 + 4 * 16 * nl + 16 + nl
    cf = np.zeros((128, NF), np.float32)
    cf[:, 0:128] = np.eye(128, dtype=np.float32)
    o = 128
    for arr in (norm_mix[:nl], norm_ffn[:nl]):
        cf[:, o:o + nl * 16] = arr.reshape(nl, 16, 128).transpose(2, 0, 1).reshape(128, nl * 16)
        o += nl * 16
    cf[:, o:o + nl * 32] = gate_bias[:nl].reshape(nl, 2, 16, 128).transpose(3, 0, 1, 2).reshape(128, nl * 32)
    o += nl * 32
    cf[:, o:o + 16] = norm_final.reshape(16, 128).T
    o += 16
    cf[:, o:o + nl] = diff_subln[:nl].T
    dl = np.ascontiguousarray(np.broadcast_to(diff_lambda[:nl].reshape(1, nl * 256), (128, nl * 256))).astype(np.float32)
    cb = np.zeros((128, 768), np.float32)
    cb[:, 0:128] = 1.0
    pa = np.zeros((128, 128), np.float32)
    for off in (0, 64):
        for i in range(8):
            pa[off + i + 8, off + i] = -1.0
            pa[off + i, off + i + 8] = 1.0
    pb = np.zeros((128, 128), np.float32)
    for i in range(16):
        pb[i + 16, i] = -1.0
        pb[i, i + 16] = 1.0
    cb[:, 128:256] = pa
    cb[:, 256:384] = pb
    kk = np.arange(128)[:, None]
    ql = np.arange(128)[None, :]
    cb[:, 384:512] = (kk >= ql)
    cb[:, 512:640] = (kk <= ql)
    cb[0:64, 640:768] = ((kk[0:64] + 64) >= ql)
    rope = np.zeros((4, 128, S), np.float32)
    rope[0] = 1.0
    rope[2] = 1.0
    pos = np.arange(S, dtype=np.float32)
    for ti, rot, offs in ((0, 16, (0, 64)), (2, 32, (0,))):
        inv = (np.float32(THETA) ** (-(np.arange(0, rot, 2, dtype=np.float32) / np.float32(rot)))).astype(np.float32)
        ang = (pos[:, None] * inv[None, :]).astype(np.float32)
        c, s_ = np.cos(ang).astype(np.float32).T, np.sin(ang).astype(np.float32).T
        half = rot // 2
        for off in offs:
            rope[ti, off:off + half] = c
            rope[ti, off + half:off + rot] = c
            rope[ti + 1, off:off + half] = s_
            rope[ti + 1, off + half:off + rot] = s_
    return cf, dl, cb, rope


def host_weights(nl, w_in, w_proj_a, w_proj_b, w_out, w_ffn_in, w_ffn_out):
    w_inF = np.empty((nl, NQK + NG, 128, 2048), np.float32)
    w_v = np.empty((nl, 5, 128, 8192), np.float32)
    w_pab = np.empty((nl, 16, 128, 1536), np.float32)
    w_o = np.empty((nl, 16, 128, 2048), np.float32)
    w_f1 = np.empty((nl, FC, 128, 4096), np.float32)
    w_f2 = np.empty((nl, 16, 128, FC * 128), np.float32)
    for l in range(nl):
        W = w_in[l]
        w_inF[l, 0:8] = relayout(W, 0, 1024)
        w_inF[l, 8:16] = relayout(W, 1024, 1024)
        w_inF[l, 16:28] = relayout(W, 3072, 1536)
        w_inF[l, 28:40] = relayout(W, 4608, 1536)
        w_inF[l, 40:72] = relayout(W, 7680, 4096)
        vcols = np.concatenate([W[:, 2048:3072], W[:, 6144:7680]], axis=1)
        w_v[l] = np.ascontiguousarray(vcols.reshape(16, 128, 5, 512).transpose(2, 1, 0, 3)).reshape(5, 128, 8192)
        w_pab[l, :, :, 0:1024] = relayout(w_proj_a[l], 0, 2048)
        w_pab[l, :, :, 1024:1536] = relayout(w_proj_b[l], 0, 2048)
        w_o[l] = relayout(w_out[l], 0, 2048)
        w_f1[l, :, :, 0:2048] = relayout(w_ffn_in[l], 0, FF)
        w_f1[l, :, :, 2048:4096] = relayout(w_ffn_in[l], FF, FF)
        w_f2[l] = relayout(w_ffn_out[l], 0, 2048)
    return dict(w_inF=w_inF, w_v=w_v, w_pab=w_pab, w_o=w_o, w_f1=w_f1, w_f2=w_f2)


_PROG_CACHE = {}


def run_cores(xs_per_core, nl, nseq, weights, consts):
    key = (nl, nseq)
    if key not in _PROG_CACHE:
        _PROG_CACHE[key] = Prog(nl, nseq).build()
    nc = _PROG_CACHE[key]
    cf, dl, cb, rope = consts
    in_maps = []
    for xc in xs_per_core:
        m = dict(x=np.ascontiguousarray(xc, dtype=np.float32), c_f32=cf, c_dlam=dl, c_bf=cb, c_rope=rope)
        m.update(weights)
        in_maps.append(m)
    res = run_bass_kernel_spmd(nc, in_maps, core_ids=list(range(len(in_maps))))
    return [r["y"] for r in res.results]


def kernel(x_prompt, x_sample, norm_mix, norm_ffn, w_in, gate_bias, diff_lambda, diff_subln,
           w_proj_a, w_proj_b, w_out, w_ffn_in, w_ffn_out, norm_final):
    f = lambda a: np.asarray(a, dtype=np.float32)
    x_prompt, x_sample = f(x_prompt), f(x_sample)
    weights = host_weights(NL, f(w_in), f(w_proj_a), f(w_proj_b), f(w_out), f(w_ffn_in), f(w_ffn_out))
    consts = host_consts(NL, f(norm_mix), f(norm_ffn), f(norm_final), f(gate_bias), f(diff_lambda), f(diff_subln))
    full = [0, 1, 4, 5]
    half = [2, 3, 6, 7]
    xs = [None] * 8
    for i, c in enumerate(full):
        xs[c] = x_prompt[2 * i:2 * i + 2]
    zero = np.zeros_like(x_sample[0])
    for i, c in enumerate(half):
        xs[c] = np.stack([x_sample[i], zero])
    ys = run_cores(xs, NL, 2, weights, consts)
    y_prompt = np.concatenate([ys[c] for c in full], axis=0)
    y_sample = np.stack([ys[c][0] for c in half], axis=0)
    return (y_prompt.astype(np.float32), y_sample.astype(np.float32))
```
